# Optimizing a Trainium2 kernel written in Bass

```python
import math
import jax, jax.numpy as jnp
from jax import lax
import numpy as np

D_MODEL = 1024
BATCH = 4
SEQ = 8192
DEPTH = 2
DEC_BATCH = 128
DEC_SEQ = 4
PAST_LEN = 16384
PAGE_SIZE = 128

HEAD_DIM = 64
N_Q_HEADS = 8
N_KV_HEADS = 2
Q_PER_KV = N_Q_HEADS // N_KV_HEADS
ATTN_WIDTH = N_Q_HEADS * HEAD_DIM
KV_WIDTH = N_KV_HEADS * HEAD_DIM
WINDOW = 128
ROPE_THETA = 10000.0
ATTN_SCALE = HEAD_DIM ** -0.5

SSM_HEAD_DIM = 64
N_SSM_HEADS = 16
SSM_WIDTH = N_SSM_HEADS * SSM_HEAD_DIM
N_SSM_GROUPS = 2
HEADS_PER_GROUP = N_SSM_HEADS // N_SSM_GROUPS
D_STATE = 128
CONV_W = 4
CONV_DIM = SSM_WIDTH + 2 * N_SSM_GROUPS * D_STATE
SSD_CHUNK = 128

MIX_WIDTH = ATTN_WIDTH + SSM_WIDTH
SPLITS = [ATTN_WIDTH,
          ATTN_WIDTH + KV_WIDTH,
          ATTN_WIDTH + 2 * KV_WIDTH,
          ATTN_WIDTH + 2 * KV_WIDTH + SSM_WIDTH,
          ATTN_WIDTH + 2 * KV_WIDTH + SSM_WIDTH + CONV_DIM]
IN_PROJ_WIDTH = ATTN_WIDTH + 2 * KV_WIDTH + SSM_WIDTH + CONV_DIM + N_SSM_HEADS

D_FF = -(-8 * D_MODEL // (3 * 256)) * 256
EPS = 1e-6

kernel_name = "hymba_swa_sink_ssd_step"


def rms_norm(x, g):
    xf = x.astype(jnp.float32)
    y = xf * lax.rsqrt(jnp.mean(xf * xf, axis=-1, keepdims=True) + EPS)
    return (y * g.astype(jnp.float32)).astype(x.dtype)


def rope(x, pos):
    half = HEAD_DIM // 2
    inv = ROPE_THETA ** (-jnp.arange(half, dtype=jnp.float32) / half)
    ang = pos.astype(jnp.float32)[:, None] * inv[None, :]
    cos = jnp.cos(ang)[None, :, None, :]
    sin = jnp.sin(ang)[None, :, None, :]
    xf = x.astype(jnp.float32)
    x1, x2 = xf[..., :half], xf[..., half:]
    return jnp.concatenate([x1 * cos - x2 * sin, x2 * cos + x1 * sin], axis=-1).astype(x.dtype)


def sink_softmax(s, sinks):
    sk = sinks.astype(jnp.float32).reshape(N_KV_HEADS, Q_PER_KV)[:, :, None, None]
    m = jnp.maximum(jnp.max(s, axis=-1, keepdims=True), sk)
    p = jnp.exp(s - m)
    return p / (jnp.sum(p, axis=-1, keepdims=True) + jnp.exp(sk - m))


def banded_window_attention(q, k, v, sinks):
    b, S = q.shape[:2]
    nb = S // WINDOW
    qb = q.reshape(b, nb, WINDOW, N_KV_HEADS, Q_PER_KV, HEAD_DIM)
    kb = k.reshape(b, nb, WINDOW, N_KV_HEADS, HEAD_DIM)
    vb = v.reshape(b, nb, WINDOW, N_KV_HEADS, HEAD_DIM)

    def with_prev(t):
        prev = jnp.pad(t[:, :-1], ((0, 0), (1, 0), (0, 0), (0, 0), (0, 0)))
        return jnp.concatenate([prev, t], axis=2)

    kk, vv = with_prev(kb), with_prev(vb)
    s = jnp.einsum('bnqgrd,bnkgd->bngrqk', qb, kk,
                   preferred_element_type=jnp.float32) * ATTN_SCALE
    qi = jnp.arange(WINDOW)[:, None]
    kj = jnp.arange(2 * WINDOW)[None, :]
    diff = qi + WINDOW - kj
    band = (diff >= 0) & (diff < WINDOW)
    mask = band[None] & ((jnp.arange(nb)[:, None, None] > 0) | (kj[None] >= WINDOW))
    s = jnp.where(mask[:, None, None], s, -jnp.inf)
    p = sink_softmax(s, sinks)
    o = jnp.einsum('bngrqk,bnkgd->bnqgrd', p.astype(v.dtype), vv)
    return o.reshape(b, S, ATTN_WIDTH)


def cached_window_attention(q, k, v, k_cache, v_cache, sinks, pos):
    b, T = q.shape[:2]
    C = k_cache.shape[1]
    kk = jnp.concatenate([k_cache.astype(k.dtype), k], axis=1)
    vv = jnp.concatenate([v_cache.astype(v.dtype), v], axis=1)
    qg = q.reshape(b, T, N_KV_HEADS, Q_PER_KV, HEAD_DIM)
    s = jnp.einsum('btgrd,bkgd->bgrtk', qg, kk,
                   preferred_element_type=jnp.float32) * ATTN_SCALE
    k_pos = jnp.concatenate([pos[0] - C + jnp.arange(C, dtype=jnp.int32), pos])
    diff = pos[:, None] - k_pos[None, :]
    mask = (diff >= 0) & (diff < WINDOW)
    s = jnp.where(mask, s, -jnp.inf)
    p = sink_softmax(s, sinks)
    o = jnp.einsum('bgrtk,bkgd->btgrd', p.astype(v.dtype), vv)
    return o.reshape(b, T, ATTN_WIDTH), kk[:, T:], vv[:, T:]


def causal_conv(xbc, conv_state, w, bias):
    L = xbc.shape[1]
    xp = jnp.concatenate([conv_state.astype(xbc.dtype), xbc], axis=1)
    out = sum(xp[:, i:i + L] * w[i] for i in range(CONV_W)) + bias
    return jax.nn.silu(out), xp[:, L:]


def ssd_scan(x, dt, A, Bm, Cm, h0):
    b, L = x.shape[:2]
    Q = min(SSD_CHUNK, L)
    pad = (-L) % Q
    xf = x.astype(jnp.float32)
    Bf = Bm.astype(jnp.float32)
    Cf = Cm.astype(jnp.float32)
    if pad:
        xf = jnp.pad(xf, ((0, 0), (0, pad), (0, 0), (0, 0)))
        dt = jnp.pad(dt, ((0, 0), (0, pad), (0, 0)))
        Bf = jnp.pad(Bf, ((0, 0), (0, pad), (0, 0), (0, 0)))
        Cf = jnp.pad(Cf, ((0, 0), (0, pad), (0, 0), (0, 0)))
    c = (L + pad) // Q
    G, Hg, P, N = N_SSM_GROUPS, HEADS_PER_GROUP, SSM_HEAD_DIM, D_STATE
    xc = xf.reshape(b, c, Q, G, Hg, P)
    dtc = dt.reshape(b, c, Q, G, Hg)
    Bc = Bf.reshape(b, c, Q, G, N)
    Cc = Cf.reshape(b, c, Q, G, N)
    acum = jnp.cumsum(dtc * A.astype(jnp.float32).reshape(G, Hg), axis=2)
    acT = jnp.moveaxis(acum, 2, -1)
    seg = acT[..., :, None] - acT[..., None, :]
    causal = jnp.tril(jnp.ones((Q, Q), dtype=bool))
    decay = jnp.exp(jnp.where(causal, seg, -jnp.inf))
    cb = jnp.einsum('bcqgn,bcsgn->bcgqs', Cc, Bc)
    xdt = xc * dtc[..., None]
    y_intra = jnp.einsum('bcghqs,bcsghp->bcqghp', cb[:, :, :, None] * decay, xdt)
    decay_end = jnp.exp(acum[:, :, -1:] - acum)
    states = jnp.einsum('bcsgn,bcsghp->bcghpn', Bc, xdt * decay_end[..., None])
    chunk_decay = jnp.exp(acum[:, :, -1])

    def step(h, inp):
        st, dc = inp
        return h * dc[..., None, None] + st, h

    h_init = h0.astype(jnp.float32).reshape(b, G, Hg, P, N)
    h_final, h_starts = lax.scan(step, h_init,
                                 (jnp.moveaxis(states, 1, 0), jnp.moveaxis(chunk_decay, 1, 0)))
    h_starts = jnp.moveaxis(h_starts, 0, 1)
    y_inter = jnp.einsum('bcqgn,bcghpn->bcqghp', Cc, h_starts) * jnp.exp(acum)[..., None]
    y = (y_intra + y_inter).reshape(b, c * Q, N_SSM_HEADS, P)[:, :L]
    return y, h_final.reshape(b, N_SSM_HEADS, P, N)


def hybrid_layer(x, pos, win_kv, conv_state, ssm_state, p):
    b, L, _ = x.shape
    u = rms_norm(x, p['norm_mix'])
    proj = u @ p['w_in']
    q, k, v, z, xbc, dt_raw = jnp.split(proj, SPLITS, axis=-1)
    q = rope(rms_norm(q.reshape(b, L, N_Q_HEADS, HEAD_DIM), p['q_norm']), pos)
    k = rope(rms_norm(k.reshape(b, L, N_KV_HEADS, HEAD_DIM), p['k_norm']), pos)
    v = v.reshape(b, L, N_KV_HEADS, HEAD_DIM)
    if win_kv is None:
        o_attn = banded_window_attention(q, k, v, p['sinks'])
        n_keep = min(WINDOW, L)
        new_k, new_v = k[:, L - n_keep:], v[:, L - n_keep:]
    else:
        o_attn, new_k, new_v = cached_window_attention(q, k, v, win_kv[0], win_kv[1],
                                                       p['sinks'], pos)
    xbc_act, new_conv = causal_conv(xbc, conv_state, p['conv_w'], p['conv_b'])
    xs, Bm, Cm = jnp.split(xbc_act, [SSM_WIDTH, SSM_WIDTH + N_SSM_GROUPS * D_STATE], axis=-1)
    dt = jax.nn.softplus(dt_raw.astype(jnp.float32) + p['dt_bias'].astype(jnp.float32))
    A = -jnp.exp(p['a_log'].astype(jnp.float32))
    xs_h = xs.reshape(b, L, N_SSM_HEADS, SSM_HEAD_DIM)
    y, new_h = ssd_scan(xs_h, dt, A,
                        Bm.reshape(b, L, N_SSM_GROUPS, D_STATE),
                        Cm.reshape(b, L, N_SSM_GROUPS, D_STATE), ssm_state)
    y = y + p['d_skip'].astype(jnp.float32)[:, None] * xs_h.astype(jnp.float32)
    yg = (y.reshape(b, L, SSM_WIDTH) * jax.nn.silu(z.astype(jnp.float32))).reshape(
        b, L, N_SSM_GROUPS, SSM_WIDTH // N_SSM_GROUPS)
    yg = yg * lax.rsqrt(jnp.mean(yg * yg, axis=-1, keepdims=True) + EPS)
    y_ssm = (yg.reshape(b, L, SSM_WIDTH) * p['ssm_norm'].astype(jnp.float32)).astype(x.dtype)
    x = x + jnp.concatenate([o_attn, y_ssm], axis=-1) @ p['w_out']
    hn = rms_norm(x, p['norm_ffn'])
    g, up = jnp.split(hn @ p['w_gate_up'], [D_FF], axis=-1)
    x = x + (jax.nn.silu(g) * up) @ p['w_down']
    return x, new_k, new_v, new_conv, new_h.astype(ssm_state.dtype)


def setup_inputs(seed: int = 0) -> dict:
    key = jax.random.key(seed)
    ks = jax.random.split(key, 24)
    f32 = jnp.float32

    def nrm(k, shape, scale):
        return jax.random.normal(k, shape, f32) * scale

    n_win = min(WINDOW, PAST_LEN)
    dt0 = jnp.exp(jax.random.uniform(ks[10], (DEPTH, N_SSM_HEADS), f32)
                  * (math.log(0.1) - math.log(0.001)) + math.log(0.001))
    return {
        'x_prompt': nrm(ks[0], (BATCH, SEQ, D_MODEL), 1.0),
        'x_sample': nrm(ks[1], (DEC_BATCH, DEC_SEQ, D_MODEL), 1.0),
        'cache_win_k': nrm(ks[2], (DEPTH, DEC_BATCH, n_win, N_KV_HEADS, HEAD_DIM), 1.0),
        'cache_win_v': nrm(ks[3], (DEPTH, DEC_BATCH, n_win, N_KV_HEADS, HEAD_DIM), 1.0),
        'state_conv': nrm(ks[4], (DEPTH, DEC_BATCH, CONV_W - 1, CONV_DIM), 1.0),
        'state_ssm': nrm(ks[5], (DEPTH, DEC_BATCH, N_SSM_HEADS, SSM_HEAD_DIM, D_STATE), 0.5),
        'norm_mix': 1.0 + nrm(ks[6], (DEPTH, D_MODEL), 0.02),
        'w_in': nrm(ks[7], (DEPTH, D_MODEL, IN_PROJ_WIDTH), D_MODEL ** -0.5),
        'q_norm': 1.0 + nrm(ks[8], (DEPTH, HEAD_DIM), 0.02),
        'k_norm': 1.0 + nrm(ks[9], (DEPTH, HEAD_DIM), 0.02),
        'sinks': nrm(ks[11], (DEPTH, N_Q_HEADS), 0.5),
        'conv_w': nrm(ks[12], (DEPTH, CONV_W, CONV_DIM), CONV_W ** -0.5),
        'conv_b': nrm(ks[13], (DEPTH, CONV_DIM), 0.01),
        'dt_bias': dt0 + jnp.log(-jnp.expm1(-dt0)),
        'a_log': jnp.log(jax.random.uniform(ks[14], (DEPTH, N_SSM_HEADS), f32, 1.0, 16.0)),
        'd_skip': 1.0 + nrm(ks[15], (DEPTH, N_SSM_HEADS), 0.1),
        'ssm_norm': 1.0 + nrm(ks[16], (DEPTH, SSM_WIDTH), 0.02),
        'w_out': nrm(ks[17], (DEPTH, MIX_WIDTH, D_MODEL), MIX_WIDTH ** -0.5),
        'norm_ffn': 1.0 + nrm(ks[18], (DEPTH, D_MODEL), 0.02),
        'w_gate_up': nrm(ks[19], (DEPTH, D_MODEL, 2 * D_FF), D_MODEL ** -0.5),
        'w_down': nrm(ks[20], (DEPTH, D_FF, D_MODEL), D_FF ** -0.5),
    }


def reference(x_prompt, x_sample, cache_win_k, cache_win_v, state_conv, state_ssm,
              norm_mix, w_in, q_norm, k_norm, sinks, conv_w, conv_b, dt_bias, a_log,
              d_skip, ssm_norm, w_out, norm_ffn, w_gate_up, w_down):
    b_p = x_prompt.shape[0]
    pos_p = jnp.arange(x_prompt.shape[1], dtype=jnp.int32)
    pos_s = PAST_LEN + jnp.arange(x_sample.shape[1], dtype=jnp.int32)
    conv0 = jnp.zeros((b_p, CONV_W - 1, CONV_DIM), x_prompt.dtype)
    h0 = jnp.zeros((b_p, N_SSM_HEADS, SSM_HEAD_DIM, D_STATE), state_ssm.dtype)

    xp, xs = x_prompt, x_sample
    pk, pv, pc, ph = [], [], [], []
    sk, sv, sc, sh = [], [], [], []
    for l in range(DEPTH):
        p = {'norm_mix': norm_mix[l], 'w_in': w_in[l], 'q_norm': q_norm[l],
             'k_norm': k_norm[l], 'sinks': sinks[l], 'conv_w': conv_w[l],
             'conv_b': conv_b[l], 'dt_bias': dt_bias[l], 'a_log': a_log[l],
             'd_skip': d_skip[l], 'ssm_norm': ssm_norm[l], 'w_out': w_out[l],
             'norm_ffn': norm_ffn[l], 'w_gate_up': w_gate_up[l], 'w_down': w_down[l]}
        xp, k1, v1, c1, h1 = hybrid_layer(xp, pos_p, None, conv0, h0, p)
        xs, k2, v2, c2, h2 = hybrid_layer(xs, pos_s, (cache_win_k[l], cache_win_v[l]),
                                          state_conv[l], state_ssm[l], p)
        pk.append(k1); pv.append(v1); pc.append(c1); ph.append(h1)
        sk.append(k2); sv.append(v2); sc.append(c2); sh.append(h2)

    return (xp, xs,
            jnp.stack(pk), jnp.stack(pv), jnp.stack(pc), jnp.stack(ph),
            jnp.stack(sk), jnp.stack(sv), jnp.stack(sc), jnp.stack(sh))
```

```python
import contextlib
import numpy as np
import ml_dtypes
import concourse.bass as bass
import concourse.mybir as mybir
from concourse.bass_utils import run_bass_kernel_spmd

F32 = mybir.dt.float32
BF16 = mybir.dt.bfloat16
AF = mybir.ActivationFunctionType
ALU = mybir.AluOpType
AX = mybir.AxisListType
ENGS = ["tensor", "vector", "scalar", "gpsimd", "sync"]


class V:
    __slots__ = ("t", "ap")

    def __init__(self, t, ap):
        self.t = t
        self.ap = ap


class T:
    def __init__(self, h, name, space):
        self.h = h
        self.name = name
        self.space = space
        self.w = {}
        self.r = {}
        self.dcnt = 0

    def __getitem__(self, idx):
        return V(self, self.h[idx])

    def v(self, ap):
        return V(self, ap)


class Prog:
    def __init__(self, nc, stack):
        self.nc = nc
        self.stack = stack
        self.q = {e: [] for e in ENGS}
        self.cnt = {e: 0 for e in ENGS}
        self.known = {e: {} for e in ENGS}
        self.sems = {}
        self.nsem = 0
        self.dma_owners = []

    def sbuf(self, name, shape, dt):
        h = self.stack.enter_context(self.nc.sbuf_tensor(name, list(shape), dt))
        return T(h, name, "sbuf")

    def psum(self, name, shape, dt):
        h = self.stack.enter_context(self.nc.psum_tensor(name, list(shape), dt))
        return T(h, name, "psum")

    def dram(self, name, shape, dt, kind):
        h = self.nc.dram_tensor(name, list(shape), dt, kind=kind).ap()
        return T(h, name, "dram")

    def semh(self, key):
        if key not in self.sems:
            self.nsem += 1
            nm = key if isinstance(key, str) else "d_" + key.name
            self.sems[key] = self.stack.enter_context(self.nc.semaphore("s_" + nm))
        return self.sems[key]

    def _wait(self, eng, key, val):
        if val <= 0 or self.known[eng].get(key, 0) >= val:
            return
        self.known[eng][key] = val
        sem = self.semh(key)
        self.q[eng].append(lambda e, sem=sem, val=val: e.wait_ge(sem, val))

    def _deps(self, eng, reads, writes, skipw=None):
        for v in reads:
            for key, val in v.t.w.items():
                if key == eng and eng == "tensor":
                    continue
                self._wait(eng, key, val)
            if v.t.space == "psum":
                for key, val in v.t.r.items():
                    if key != eng:
                        self._wait(eng, key, val)
        for v in writes:
            for key, val in v.t.w.items():
                if (key == eng and eng == "tensor") or key is skipw:
                    continue
                self._wait(eng, key, val)
            for key, val in v.t.r.items():
                if key == eng and eng == "tensor":
                    continue
                self._wait(eng, key, val)

    def op(self, eng, meth, reads, writes, *args, **kw):
        reads = list(reads) + list(kw.pop("xr", []))
        writes = list(writes) + list(kw.pop("xw", []))
        self._deps(eng, reads, writes)
        self.cnt[eng] += 1
        c = self.cnt[eng]
        for v in reads:
            v.t.r[eng] = c
        for v in writes:
            v.t.w[eng] = c
        sem = self.semh(eng)
        kw2 = {k: (x.ap if isinstance(x, V) else x) for k, x in kw.items()}
        self.q[eng].append(
            lambda e, meth=meth, kw2=kw2, sem=sem: getattr(e, meth)(**kw2).then_inc(sem, 1)
        )

    def dma(self, eng, out, in_, owner=None, **kw):
        if owner is None:
            owner = out.t if out.t.space == "sbuf" else in_.t
        self._deps(eng, [in_], [out], skipw=owner)
        if owner.dcnt == 0:
            self.dma_owners.append(owner)
        owner.dcnt += 16
        in_.t.r[owner] = owner.dcnt
        out.t.w[owner] = owner.dcnt
        sem = self.semh(owner)
        oa, ia = out.ap, in_.ap
        self.q[eng].append(
            lambda e, oa=oa, ia=ia, sem=sem, kw=kw: e.dma_start(out=oa, in_=ia, **kw).then_inc(sem, 16)
        )

    def cnt_total(self):
        return sum(self.cnt.values()) + sum(o.dcnt for o in self.dma_owners) + len(self.q["sync"])

    def finish(self):
        for o in self.dma_owners:
            self._wait("sync", o, o.dcnt)
        for e in ENGS:
            if e != "sync" and self.cnt[e] > 0:
                self._wait("sync", e, self.cnt[e])

    def emit(self):
        self.finish()
        with self.nc.Block() as block:
            for e in ENGS:
                ql = self.q[e]
                if not ql:
                    continue

                def section(eng, ql=ql):
                    for f in ql:
                        f(eng)

                getattr(block, e)(section)

    def mm(self, out, lhsT, rhs, start=True, stop=True, **kw):
        self.op("tensor", "matmul", [lhsT, rhs], [out], out=out, lhsT=lhsT, rhs=rhs,
                start=start, stop=stop, **kw)

    def tr(self, out, in_, ident):
        self.op("tensor", "transpose", [in_, ident], [out], out=out, in_=in_, identity=ident)

    def act(self, out, in_, func, bias=None, scale=None, accum_out=None, eng="scalar"):
        reads = [in_]
        kw = dict(out=out, in_=in_, func=func)
        if bias is not None:
            kw["bias"] = bias
            if isinstance(bias, V):
                reads.append(bias)
        if scale is not None:
            kw["scale"] = scale
            if isinstance(scale, V):
                reads.append(scale)
        writes = [out]
        if accum_out is not None:
            kw["accum_out"] = accum_out
            writes.append(accum_out)
        self.op("scalar", "activation", reads, writes, **kw)

    def tt(self, out, in0, in1, op, eng="vector", **kw):
        self.op(eng, "tensor_tensor", [in0, in1], [out], out=out, in0=in0, in1=in1, op=op, **kw)

    def ts(self, out, in0, s1, op0, s2=None, op1=None, eng="vector"):
        reads = [in0] + [x for x in (s1, s2) if isinstance(x, V)]
        kw = dict(out=out, in0=in0, scalar1=s1, scalar2=s2, op0=op0)
        if op1 is not None:
            kw["op1"] = op1
        self.op(eng, "tensor_scalar", reads, [out], **kw)

    def stt(self, out, in0, scalar, in1, op0, op1):
        reads = [in0, in1] + ([scalar] if isinstance(scalar, V) else [])
        self.op("vector", "scalar_tensor_tensor", reads, [out], out=out, in0=in0, scalar=scalar,
                in1=in1, op0=op0, op1=op1)

    def cp(self, out, in_, eng="vector"):
        if eng == "scalar":
            self.act(out, in_, AF.Copy)
        else:
            self.op(eng, "tensor_copy", [in_], [out], out=out, in_=in_)


D = 1024
HD = 64
NH = 16
CONVD = 1536
DFF = 2816
INW = 3344
NL = 2
EPS = 1e-6
ATT_SCALE = HD ** -0.5
PAST = 16384

PV_GMIX, PV_GFFN, PV_QG, PV_KG, PV_SINK, PV_CW, PV_CB, PV_DTB, PV_ALOG, PV_DSK, PV_SSMN = (
    0, 8, 16, 17, 18, 26, 74, 86, 102, 118, 134)
NPV = 142
CB_ID, CB_TRI, CB_ONES, CB_NEG4, CB_MPREV, CB_MCUR, CB_BLK, CB_ROT, CB_NEGP4 = 0, 128, 256, 384, 896, 1024, 1152, 1280, 1408
NCB = 1920
NCF = 131


def in_pieces():
    return [("X0", 1792, 512), ("X1", 2304, 512), ("X2", 2816, 512), ("DT", 3328, 16),
            ("Q", 0, 512), ("KV", 512, 256), ("Z0", 768, 512), ("Z1", 1280, 512)]


def layer_pieces():
    pcs = [(n, "w_in", 8, c0, nc_) for (n, c0, nc_) in in_pieces()]
    pcs += [("O%d" % m, "w_out", 12, m * 256, 256) for m in range(4)]
    for j in range(6):
        c0 = j * 512
        w = min(512, DFF - c0)
        pcs.append(("G%d" % j, "w_gu", 8, c0, w))
        pcs.append(("U%d" % j, "w_gu", 8, DFF + c0, w))
    pcs += [("N%d" % m, "w_dn", 22, m * 128, 128) for m in range(8)]
    return pcs


def build(NPT, NST, NSEQ):
    nc = bass.Bass("TRN2", target_bir_lowering=False)
    NT = NPT + NST
    NTOK = NPT * 512 + NSEQ * 4
    NSBMAX = max(4, NSEQ)
    assert NST == 1
    with contextlib.ExitStack() as st:
        P = Prog(nc, st)
        xin = P.dram("xT_in", [D, NTOK], F32, "ExternalInput")
        cosd = P.dram("cosT", [128, NTOK], F32, "ExternalInput")
        sind = P.dram("sinT", [128, NTOK], F32, "ExternalInput")
        ckd = P.dram("cache_k", [NL, NSEQ, 128, 128], F32, "ExternalInput")
        cvd = P.dram("cache_v", [NL, NSEQ, 128, 128], F32, "ExternalInput")
        scd = P.dram("st_conv", [NL, NSEQ, 3, CONVD], F32, "ExternalInput")
        ssd = P.dram("st_ssm", [NL, NSEQ, 1024, 128], F32, "ExternalInput")
        wd = {"w_in": P.dram("w_in", [NL, D, INW], F32, "ExternalInput"),
              "w_out": P.dram("w_out", [NL, CONVD, D], F32, "ExternalInput"),
              "w_gu": P.dram("w_gu", [NL, D, 2 * DFF], F32, "ExternalInput"),
              "w_dn": P.dram("w_dn", [NL, DFF, D], F32, "ExternalInput")}
        pvd = P.dram("pvec", [NL, 128, NPV], F32, "ExternalInput")
        cbd = P.dram("cbf", [128, NCB], BF16, "ExternalInput")
        cfd = P.dram("cf32", [128, NCF], F32, "ExternalInput")
        yout = P.dram("yT", [D, NTOK], F32, "ExternalOutput")
        o_pk = P.dram("o_pk", [NL, 128, 128], F32, "ExternalOutput")
        o_pv = P.dram("o_pv", [NL, 128, 128], F32, "ExternalOutput")
        o_pc = P.dram("o_pc", [NL, 3, CONVD], F32, "ExternalOutput")
        o_ph = P.dram("o_ph", [NL, 1024, 128], F32, "ExternalOutput")
        o_sk = P.dram("o_sk", [NL, NSEQ, 128, 128], F32, "ExternalOutput")
        o_sv = P.dram("o_sv", [NL, NSEQ, 128, 128], F32, "ExternalOutput")
        o_sc = P.dram("o_sc", [NL, NSEQ, 3, CONVD], F32, "ExternalOutput")
        o_sh = P.dram("o_sh", [NL, NSEQ, 1024, 128], F32, "ExternalOutput")

        pcs = layer_pieces()
        offs = []
        tot = 0
        for (_, _, nk, _, ncol) in pcs:
            offs.append(tot)
            tot += nk * ncol
        wscr = [P.dram("wscr%d" % l, [128, tot], BF16, "Internal") for l in range(NL)]

        xTk = [P.sbuf("xT%d" % k, [128, 512], F32) for k in range(8)]
        uT = P.sbuf("uT", [128, 8, 512], BF16)
        sq = [P.sbuf("sq%d" % i, [128, 512], BF16) for i in range(2)]
        rs = P.sbuf("rs", [128, 512], F32)
        rstok = P.sbuf("rstok", [128, 16], F32)
        t1 = P.sbuf("t1", [128, 512], F32)
        t2 = P.sbuf("t2", [128, 512], F32)
        qn = P.sbuf("qn", [128, 512], BF16)
        t1b = P.sbuf("t1b", [128, 512], F32)
        t2b = P.sbuf("t2b", [128, 512], F32)
        qnb = P.sbuf("qnb", [128, 512], BF16)
        qr = P.sbuf("qr", [128, 4, 512], BF16)
        krc = P.sbuf("krc", [128, 4, 128], BF16)
        krp = P.sbuf("krp", [128, NSBMAX, 128], BF16)
        vcur = P.sbuf("vcur", [128, NSBMAX, 2, 65], BF16)
        vprev = P.sbuf("vprev", [128, NSBMAX, 2, 65], BF16)
        vout = P.sbuf("vout", [128, 128], F32)
        kout = P.sbuf("kout", [128, 128], F32)
        carry_k = [P.sbuf("carry_k%d" % l, [128, 128], BF16) for l in range(NL)]
        carry_v = [P.sbuf("carry_v%d" % l, [128, 2, 65], BF16) for l in range(NL)]
        ctail = [P.sbuf("ctail%d" % l, [128, 12, 3], BF16) for l in range(NL)]
        hT = [P.sbuf("hT%d" % l, [128, 1024], F32) for l in range(NL)]
        hTb = [P.sbuf("hTb%d" % l, [128, 1024], BF16) for l in range(NL)]
        dtraw = P.sbuf("dtraw", [128, NSBMAX, 16], F32)
        arena = P.sbuf("arena", [128, 12432], BF16)
        mixT = P.sbuf("mixT", [128, 12, 512], BF16)
        wbuf = [P.sbuf("wbuf%d" % i, [128, 4096], BF16) for i in range(3)]
        diagw = P.sbuf("diagw", [128, 48, 128], BF16)
        cs = P.sbuf("cs", [128, 512], F32)
        sn = P.sbuf("sn", [128, 512], F32)
        cb = P.sbuf("cb", [128, NCB], BF16)
        cf = P.sbuf("cf", [128, NCF], F32)
        pv = [P.sbuf("pv%d" % l, [128, NPV], F32) for l in range(NL)]
        esink = [P.sbuf("esink%d" % l, [128, 8], F32) for l in range(NL)]
        Aneg = [P.sbuf("Aneg%d" % l, [128, 16], F32) for l in range(NL)]
        pT = [P.sbuf("pTs%d" % i, [128, 512], BF16) for i in range(2)]
        on = P.sbuf("on", [128, 512], BF16)
        den = P.sbuf("den", [128, 8], F32)
        def ssd_set(i):
            d = {}
            for nm in ("e_", "dtt", "a_", "nacum", "eacum", "dend", "wdt", "cdec"):
                d[nm] = P.sbuf("%s%d" % (nm, i), [128, 16], F32)
            d["ahi"] = P.sbuf("ahi%d" % i, [128, 16], BF16)
            d["alo"] = P.sbuf("alo%d" % i, [128, 16], BF16)
            d["xdt"] = P.sbuf("xdt%d" % i, [128, 1024], BF16)
            d["xdw"] = P.sbuf("xdw%d" % i, [128, 1024], BF16)
            d["xD"] = P.sbuf("xD%d" % i, [128, 1024], BF16)
            d["zs"] = P.sbuf("zs%d" % i, [128, 1024], BF16)
            d["Btok"] = P.sbuf("Btok%d" % i, [128, 256], BF16)
            d["cbT"] = P.sbuf("cbT%d" % i, [128, 256], F32)
            d["dT"] = [P.sbuf("dT%d_%d" % (i, q), [128, 512], F32) for q in range(2)]
            d["MT"] = P.sbuf("MT%d" % i, [128, 16, 128], BF16)
            d["ytmp"] = [P.sbuf("ytmp%d_%d" % (i, g), [128, 512], F32) for g in range(2)]
            d["ysm"] = P.sbuf("ysm%d" % i, [128, 1024], BF16)
            d["sqj"] = d["dT"][1]
            d["ss2"] = P.sbuf("ss2%d" % i, [128, 2], F32)
            return d
        ssd_sets = [ssd_set(0), ssd_set(1)]
        stg = P.sbuf("stg", [128, 512], F32)
        stc = P.sbuf("stc", [3, 512], F32)

        pA = [P.psum("pA%d" % i, [128, 512], F32) for i in range(4)]
        pO = [P.psum("pO%d" % i, [128, 512], F32) for i in range(2)]
        pX = [P.psum("pX%d" % i, [128, 512], F32) for i in range(2)]
        ring = {"a": 0}

        def nextA():
            t = pA[ring["a"] % 4]
            ring["a"] += 1
            return t

        xpre_ap = arena.h[:, 0:6288].rearrange("p (c s t) -> p c s t", c=12, s=4, t=131)
        xc_ap = arena.h[:, 6288:12432].rearrange("p (c t) -> p c t", c=12, t=512)
        act_ap = arena.h[:, 0:11264].rearrange("p (f t) -> p f t", f=22, t=512)
        A = arena.v
        arena_xc = T(arena.h, "arena_xc", "sbuf")
        AX_ = arena_xc.v
        xc_dummy = AX_(xc_ap[:, 0, 0:1])

        def C(col, n=128):
            return cb[:, col:col + n]

        identb = C(CB_ID)
        tri = C(CB_TRI)
        onesb = C(CB_ONES)
        neg4 = C(CB_NEG4, 512)
        mprev = C(CB_MPREV)
        mcur = C(CB_MCUR)
        blk = C(CB_BLK)
        rotm = C(CB_ROT)
        identf = cf[:, 0:128]
        onec = cf[:, 129:130]
        validc = cf[:, 130:131]

        def bc8(t, c0, n=8, d=64):
            return t.v(t.h[:, c0:c0 + n].unsqueeze(2).broadcast_to([128, n, d]))

        def v3(t, c0, c1, a):
            return t.v(t.h[:, c0:c1].rearrange("p (a b) -> p a b", a=a))

        def memset(tv, val):
            P.op("vector", "memset", [], [tv], ap=tv, constant=val)

        P.dma("sync", cb[:], cbd[:, :])
        P.dma("sync", cf[:], cfd[:, :])
        for l in range(NL):
            P.dma("sync", pv[l][:], pvd[l, :, :])
        for l in range(NL):
            for i, (_, wn, nk, c0, ncol) in enumerate(pcs):
                src = wd[wn].h[l, :, c0:c0 + ncol].rearrange("(k p) c -> p k c", p=128)
                dst = wscr[l].h[:, offs[i]:offs[i] + nk * ncol].rearrange("p (k c) -> p k c", k=nk)
                if pcs[i][0] == "Q":
                    s5 = src.rearrange("p k (two four d) -> p k two four d", two=2, four=4)
                    d5 = dst.rearrange("p k (four two d) -> p k two four d", two=2, four=4)
                    for tw in range(2):
                        for fo in range(4):
                            P.dma("gpsimd", wscr[l].v(d5[:, :, tw, fo, :]), wd[wn].v(s5[:, :, tw, fo, :]),
                                  owner=wscr[l])
                    continue
                P.dma("gpsimd", wscr[l].v(dst), wd[wn].v(src), owner=wscr[l])
        for l in range(NL):
            P.act(esink[l][:], pv[l][:, PV_SINK:PV_SINK + 8], AF.Exp)
            P.act(Aneg[l][:], pv[l][:, PV_ALOG:PV_ALOG + 16], AF.Exp)
            P.ts(Aneg[l][:], Aneg[l][:], -1.0, ALU.mult)
            memset(hT[l][:], 0.0)
            memset(hTb[l][:], 0.0)
            memset(ctail[l][:], 0.0)
            memset(carry_k[l][:], 0.0)
            memset(carry_v[l][:], 1.0)
        memset(vcur[:], 1.0)
        memset(vprev[:], 1.0)

        def build_diagw(lq):
            for c in range(12):
                for i in range(4):
                    col = PV_CW + c * 4 + i
                    P.act(diagw[:, c * 4 + i, :], identb, AF.Copy, scale=pv[lq][:, col:col + 1])
        build_diagw(0)

        banks_free = list(pA) + list(pO) + list(pX)
        dbgF = [None]

        def acq(n):
            while len(banks_free) < n:
                yield
            got = [banks_free.pop(0) for _ in range(n)]
            return got

        def rel(bs):
            banks_free.extend(bs)

        def run(tasks):
            active = list(tasks)
            idle_rounds = 0
            while active:
                progressed = False
                for g in list(active):
                    before = P.cnt_total()
                    try:
                        next(g)
                    except StopIteration:
                        active.remove(g)
                        progressed = True
                        continue
                    if P.cnt_total() != before:
                        progressed = True
                idle_rounds = 0 if progressed else idle_rounds + 1
                if idle_rounds > 64:
                    raise RuntimeError("emission scheduler deadlock: tasks=%s banks_free=%d F=%s wst=%s occ=%s refc=%s" % (
                        [g.__name__ for g in active], len(banks_free), dbgF[0], wst, occ,
                        {k: v for k, v in refc.items() if v > 0}))

        plan = []
        plan_idx = {}
        for t in range(NT):
            for l in range(NL):
                for i, pc_ in enumerate(pcs):
                    plan_idx[(t, l, pc_[0])] = len(plan)
                    plan.append((l, i))
        USERS = {"Q": 2, "KV": 2, "Z0": 2, "Z1": 2}
        wst = {"issued": 0}
        occ = [None] * 3
        refc = {}

        def try_prefetch(upto):
            while wst["issued"] <= min(upto, len(plan) - 1):
                jj = wst["issued"]
                b = jj % 3
                if occ[b] is not None and refc[occ[b]] > 0:
                    return
                ll, ii = plan[jj]
                n = pcs[ii][2] * pcs[ii][4]
                P.dma("sync", wbuf[b][:, 0:n], wscr[ll][:, offs[ii]:offs[ii] + n])
                occ[b] = jj
                refc[jj] = USERS.get(pcs[ii][0], 1)
                wst["issued"] += 1

        def wget(t, l, name):
            j = plan_idx[(t, l, name)]
            while True:
                try_prefetch(j + 2)
                if wst["issued"] > j:
                    break
                yield
            pi = plan[j][1]
            nk, ncol = pcs[pi][2], pcs[pi][4]
            wb = wbuf[j % 3]
            return j, wb, wb.h[:, 0:nk * ncol].rearrange("p (k c) -> p k c", k=nk)

        def wrel(j):
            refc[j] -= 1
            assert refc[j] >= 0

        def rstd_inplace(tv):
            P.act(tv, tv, AF.Ln)
            P.act(tv, tv, AF.Exp, scale=-0.5)

        krcf = krc.h[:].rearrange("p s t -> p (s t)")
        qksets = [(sq[0], t1, t2, qn), (sq[1], t1b, t2b, qnb)]

        for t in range(NT):
            is_s = t >= NPT
            nt = 4 if is_s else 128
            NSB = NSEQ if is_s else 4
            TW = NSB * nt
            W = slice(0, TW)
            TP = slice(0, nt)
            SBW = 3 + nt
            tok0 = t * 512 if not is_s else NPT * 512 + (t - NPT) * TW
            xpre_ap = arena.h[:, 0:12 * NSB * SBW].rearrange("p (c s t) -> p c s t", c=12, s=NSB, t=SBW)
            for k in range(8):
                P.dma("sync", xTk[k][:, W], xin[k * 128:(k + 1) * 128, tok0:tok0 + TW])
            P.dma("sync", cs[:, W], cosd[:, tok0:tok0 + TW])
            P.dma("sync", sn[:, W], sind[:, tok0:tok0 + TW])

            def rmsnorm(gcol, l, want_tok):
                for k in range(8):
                    P.ts(uT[:, k, W], xTk[k][:, W], pv[l][:, gcol + k:gcol + k + 1], ALU.mult)
                ps = pA[0]
                for k in range(8):
                    s = sq[k % 2]
                    P.act(s[:, W], xTk[k][:, W], AF.Square)
                    P.mm(ps[:, W], onesb, s[:, W], start=(k == 0), stop=(k == 7))
                P.act(rs[:, W], ps[:, W], AF.Ln, scale=1.0 / D, bias=cf[:, 128:129])
                P.act(rs[:, W], rs[:, W], AF.Exp, scale=-0.5)
                if want_tok:
                    pt_ = pA[1]
                    for j in range(NSB):
                        P.mm(pt_[TP, j:j + 1], rs[0:1, j * nt:(j + 1) * nt], cf[0:1, 129:130])
                    P.cp(rstok[TP, 0:NSB], pt_[TP, 0:NSB])

            for l in range(NL):
                pvl = pv[l]
                F = {"qk": 0, "v": False, "conv": False, "dt": False}
                dbgF[0] = F

                def sub_jsl(j):
                    return slice(j * nt, (j + 1) * nt)

                def emits(j):
                    return is_s or (t == NPT - 1 and j == 3)

                def bidx(j):
                    return (t - NPT) * NSB + j

                if is_s:
                    for j in range(NSB):
                        b = bidx(j)
                        P.dma("sync", stg[:, 0:128], ckd[l, b, :, :])
                        P.tr(pX[1][:, 0:128], stg[:, 0:128], identf)
                        P.cp(krp[:, j, :], pX[1][:, 0:128])
                        P.dma("sync", stg[:, 128:256], cvd[l, b, :, :])
                        P.cp(vprev.v(vprev.h[:, j, :, 0:64]), v3(stg, 128, 256, 2))
                        P.dma("sync", o_sk[l, b, 0:124, :], ckd[l, b, 4:128, :], owner=o_sk)
                        P.dma("sync", o_sv[l, b, 0:124, :], cvd[l, b, 4:128, :], owner=o_sv)
                        tp2 = pA[1]
                        for q3 in range(3):
                            P.dma("sync", stc[:], scd[l, b, :, q3 * 512:(q3 + 1) * 512])
                            for cc in range(4):
                                c = q3 * 4 + cc
                                P.mm(tp2[:, c * 3:c * 3 + 3], stc[0:3, cc * 128:(cc + 1) * 128], cf[0:3, 0:3])
                        P.cp(A(xpre_ap[:, :, j, 0:3]), v3(tp2, 0, 36, 12))
                else:
                    P.cp(krp[:, 0, :], carry_k[l][:])
                    P.cp(vprev[:, 0, :, :], carry_v[l][:])
                    P.cp(A(xpre_ap[:, :, 0, 0:3]), ctail[l][:])

                rmsnorm(PV_GMIX, l, True)

                def qk_thread(chunks, tset):
                    while not F["conv"]:
                        yield
                    bk = None
                    s, t1_, t2_, qn_ = tset
                    for ci, c in enumerate(chunks):
                        isk = (c == 4)
                        pj, wb, wv = yield from wget(t, l, "KV" if isk else "Q")
                        if bk is None:
                            bk = yield from acq(2)
                        ps = bk[0]
                        for k in range(8):
                            lh = wv[:, k, 0:128] if isk else wv[:, k, c * 128:(c + 1) * 128]
                            P.mm(ps[:, W], wb.v(lh), uT[:, k, W], start=(k == 0), stop=(k == 7))
                        if isk or ci == len([x for x in chunks if x < 4]) - 1:
                            wrel(pj)
                        yield
                        qw = t2_
                        P.tt(qw[:, W], ps[:, W], rs[:, W], ALU.mult)
                        P.act(s[:, W], qw[:, W], AF.Square)
                        ms = bk[1]
                        P.mm(ms[:, W], blk, s[:, W])
                        yield
                        P.act(t1_[:, W], ms[:, W], AF.Ln, scale=1.0 / HD, bias=cf[:, 128:129])
                        P.act(t1_[:, W], t1_[:, W], AF.Exp, scale=-0.5)
                        gcol = PV_KG if isk else PV_QG
                        P.stt(qn_[:, W], qw[:, W], pvl[:, gcol:gcol + 1], t1_[:, W], ALU.mult, ALU.mult)
                        rot = bk[1]
                        P.mm(rot[:, W], rotm, qn_[:, W])
                        yield
                        P.tt(t2_[:, W], qn_[:, W], cs[:, W], ALU.mult)
                        P.tt(t1_[:, W], rot[:, W], sn[:, W], ALU.mult)
                        outv = krc.v(krcf[:, W]) if isk else qr[:, c, W]
                        P.tt(outv, t2_[:, W], t1_[:, W], ALU.add)
                        F["qk"] += 1
                        yield
                    rel(bk)

                def bulk_thread():
                    bk = yield from acq(4)
                    ri = [0]

                    def nb():
                        ri[0] += 1
                        return bk[ri[0] % len(bk)]
                    for xi in range(3):
                        pj, wb, wv = yield from wget(t, l, "X%d" % xi)
                        for cc in range(4):
                            c = xi * 4 + cc
                            ps = nb()
                            for k in range(8):
                                P.mm(ps[:, W], wb.v(wv[:, k, cc * 128:(cc + 1) * 128]), uT[:, k, W],
                                     start=(k == 0), stop=(k == 7))
                            P.tt(A(xpre_ap[:, c, :, 3:3 + nt]), ps.v(ps.h[:, W].rearrange("p (a b) -> p a b", a=NSB)),
                                 rs.v(rs.h[:, W].rearrange("p (a b) -> p a b", a=NSB)), ALU.mult)
                            yield
                        wrel(pj)
                    pj, wb, wv = yield from wget(t, l, "DT")
                    for j in range(NSB):
                        ps = nb()
                        for k in range(8):
                            P.mm(ps[TP, 0:16], uT[:, k, sub_jsl(j)], wb.v(wv[:, k, :]), start=(k == 0), stop=(k == 7))
                        P.stt(dtraw[TP, j, :], ps[TP, 0:16], rstok[TP, j:j + 1], pvl[TP, PV_DTB:PV_DTB + 16],
                              ALU.mult, ALU.add)
                    wrel(pj)
                    F["dt"] = True
                    yield
                    if not is_s:
                        P.cp(A(xpre_ap[:, :, 1:4, 0:3]), A(xpre_ap[:, :, 0:3, nt:nt + 3]))
                        P.cp(ctail[l][:], A(xpre_ap[:, :, 3, nt:nt + 3]))
                    for j in range(NSB):
                        if not emits(j):
                            continue
                        for q3 in range(3):
                            pa = nb()
                            for cc in range(4):
                                c = q3 * 4 + cc
                                P.mm(pa[0:3, cc * 128:(cc + 1) * 128], A(xpre_ap[:, c, j, nt:nt + 3]), identb)
                            P.cp(stc[:], pa[0:3, :])
                            if is_s:
                                P.dma("sync", o_sc[l, bidx(j), :, q3 * 512:(q3 + 1) * 512], stc[:])
                            else:
                                P.dma("sync", o_pc[l, :, q3 * 512:(q3 + 1) * 512], stc[:])
                        yield
                    for c in range(12):
                        ps = nb()
                        for i in range(4):
                            P.mm(ps.v(ps.h[:, W].rearrange("p (a b) -> p a b", a=NSB)), diagw[:, c * 4 + i, :],
                                 A(xpre_ap[:, c, :, i:i + nt]), start=(i == 0), stop=(i == 3))
                        P.act(AX_(xc_ap[:, c, W]), ps[:, W], AF.Silu, bias=pvl[:, PV_CB + c:PV_CB + c + 1])
                        yield
                    F["conv"] = True
                    rel(bk[2:])
                    bk = bk[:2]
                    pj, wb, wv = yield from wget(t, l, "KV")
                    for j in range(NSB):
                        ps = nb()
                        for k in range(8):
                            P.mm(ps[TP, 0:128], uT[:, k, sub_jsl(j)], wb.v(wv[:, k, 128:256]),
                                 start=(k == 0), stop=(k == 7))
                        P.act(vcur.v(vcur.h[TP, j, :, 0:64]), ps.v(ps.h[TP, 0:128].rearrange("p (a b) -> p a b", a=2)),
                              AF.Copy, scale=rstok[TP, j:j + 1])
                        if emits(j):
                            P.act(vout[TP, :], ps[TP, 0:128], AF.Copy, scale=rstok[TP, j:j + 1])
                            if is_s:
                                P.dma("sync", o_sv[l, bidx(j), 124:128, :], vout[0:4, :])
                            else:
                                P.dma("sync", o_pv[l, :, :], vout[:])
                        yield
                    wrel(pj)
                    F["v"] = True
                    rel(bk)

                def att_thread():
                    while F["qk"] < 5 or not F["v"]:
                        yield
                    if not is_s:
                        P.cp(krp[:, 1:4, :], krc[:, 0:3, :])
                        P.cp(vprev[:, 1:4, :, :], vcur[:, 0:3, :, :])
                        P.cp(carry_k[l][:], krc[:, 3, :])
                        P.cp(carry_v[l][:], vcur[:, 3, :, :])
                    bk = yield from acq(2)
                    for j in range(NSB):
                        jsl = sub_jsl(j)
                        if emits(j):
                            pa = bk[0]
                            P.mm(pa[TP, 0:128], krc.v(krcf[:, jsl]), identb)
                            P.cp(kout[TP, :], pa[TP, 0:128], eng="scalar")
                            if is_s:
                                P.dma("sync", o_sk[l, bidx(j), 124:128, :], kout[0:4, :])
                            else:
                                P.dma("sync", o_pk[l, :, :], kout[:])
                        for g in range(2):
                            first = True
                            oacc = bk[1]
                            for which in ("prev", "cur"):
                                if which == "prev" and (not is_s) and t == 0 and j == 0:
                                    continue
                                prev = which == "prev"
                                nk_ = 128 if prev else nt
                                KP = slice(0, nk_)
                                S = bk[0]
                                pt = pT[0] if prev else pT[1]
                                klh = krp[64 * g:64 * g + 64, j, :] if prev else krc.v(krcf[64 * g:64 * g + 64, jsl])
                                S3 = S.v(S.h[KP, 0:4 * nt].rearrange("p (a b) -> p a b", a=4))
                                ncol = CB_NEGP4 if prev else CB_NEG4
                                P.mm(S3, cb[KP, CB_ID:CB_ID + nk_],
                                     cb.v(cb.h[KP, ncol:ncol + 512].rearrange("p (a b) -> p a b", a=4)[:, :, 0:nt]),
                                     start=True, stop=False)
                                P.mm(S3, klh, qr.v(qr.h[64 * g:64 * g + 64, :, jsl]), start=False, stop=True)
                                yield
                                P.act(pt[KP, 0:4 * nt], S[KP, 0:4 * nt], AF.Exp, scale=ATT_SCALE)
                                vv = vprev if prev else vcur
                                for h in range(4):
                                    P.mm(oacc[TP, h * 65:(h + 1) * 65], pt[KP, h * nt:(h + 1) * nt],
                                         vv.v(vv.h[KP, j, g, :]), start=(first and h == 0),
                                         stop=(which == "cur" and h == 3), skip_group_check=True)
                                first = False
                                yield
                            o3 = oacc.h[TP, 0:260].rearrange("p (h e) -> p h e", h=4)
                            dg = den[TP, 4 * g:4 * g + 4]
                            P.tt(dg, oacc.v(o3[:, :, 64]), esink[l][TP, 4 * g:4 * g + 4], ALU.add)
                            P.op("vector", "reciprocal", [dg], [dg], out=dg, in_=dg)
                            P.tt(on.v(on.h[TP, g * 256:(g + 1) * 256].rearrange("p (a b) -> p a b", a=4)),
                                 oacc.v(o3[:, :, 0:64]),
                                 den.v(den.h[TP, 4 * g:4 * g + 4].unsqueeze(2).broadcast_to([nt, 4, 64])), ALU.mult)
                            yield
                        tb = bk[0]
                        for c in range(4):
                            P.mm(tb[:, c * nt:(c + 1) * nt], on[TP, c * 128:(c + 1) * 128], cb[TP, CB_ID:CB_ID + nt])
                        P.cp(mixT.v(mixT.h[:, 0:4, jsl]), tb.v(tb.h[:, 0:4 * nt].rearrange("p (a b) -> p a b", a=4)),
                             eng="scalar")
                        yield
                    rel(bk)

                F["h"] = 0

                def ssd_thread(js, S):
                    while not (F["conv"] and F["dt"]):
                        yield
                    bk = yield from acq(3)
                    bA, bB, bC = bk
                    e_, dtt, a_, nacum, eacum, dend, wdt, cdec = (S[k_] for k_ in
                                                                   ("e_", "dtt", "a_", "nacum", "eacum", "dend", "wdt", "cdec"))
                    ahi, alo, xdt, xdw, xD, Btok, cbT, dTt, MT, ytmp, ysm, sqj, ss2, zs = (
                        S[k_] for k_ in ("ahi", "alo", "xdt", "xdw", "xD", "Btok", "cbT", "dT", "MT", "ytmp", "ysm",
                                         "sqj", "ss2", "zs"))
                    zpc = []
                    for zi in range(2):
                        zpc.append((yield from wget(t, l, "Z%d" % zi)))
                    identn = cb[TP, CB_ID:CB_ID + nt]
                    trin = cb[TP, CB_TRI:CB_TRI + nt]

                    def b8(tl, c0):
                        return tl.v(tl.h[TP, c0:c0 + 8].unsqueeze(2).broadcast_to([nt, 8, 64]))

                    def r8(tl, c0, c1):
                        return tl.v(tl.h[TP, c0:c1].rearrange("p (a b) -> p a b", a=8))
                    for j in js:
                        jsl = sub_jsl(j)
                        b = bidx(j) if is_s else None
                        P.act(e_[TP, :], dtraw[TP, j, :], AF.Exp)
                        P.act(dtt[TP, :], e_[TP, :], AF.Ln, bias=cf[TP, 129:130])
                        P.tt(a_[TP, :], dtt[TP, :], Aneg[l][TP, :], ALU.mult)
                        P.cp(ahi[TP, :], a_[TP, :])
                        P.tt(alo[TP, :], a_[TP, :], ahi[TP, :], ALU.subtract)
                        pc = bB
                        P.mm(pc[TP, 0:16], trin, ahi[TP, :], start=True, stop=False)
                        P.mm(pc[TP, 0:16], trin, alo[TP, :], start=False, stop=True)
                        P.mm(pc[:, 16:32], cb[TP, CB_ONES:CB_ONES + 128], ahi[TP, :], start=True, stop=False)
                        P.mm(pc[:, 16:32], cb[TP, CB_ONES:CB_ONES + 128], alo[TP, :], start=False, stop=True)
                        pb = bC
                        for g in range(2):
                            P.mm(pb[TP, g * 128:(g + 1) * 128], AX_(xc_ap[:, 8 + g, jsl]), identb)
                        for g in range(2):
                            P.mm(pb[TP, 256 + g * nt:256 + (g + 1) * nt], AX_(xc_ap[:, 8 + g, jsl]),
                                 AX_(xc_ap[:, 10 + g, jsl]))
                        for c in range(4):
                            P.mm(bA[TP, c * 128:(c + 1) * 128], AX_(xc_ap[:, c, jsl]), identb)
                        yield
                        P.ts(nacum[TP, :], pc[TP, 0:16], -1.0, ALU.mult)
                        P.act(eacum[TP, :], pc[TP, 0:16], AF.Exp)
                        P.tt(dend[TP, :], pc[TP, 16:32], nacum[TP, :], ALU.add)
                        P.act(dend[TP, :], dend[TP, :], AF.Exp)
                        P.act(cdec[:], pc[:, 16:32], AF.Exp)
                        P.tt(wdt[TP, :], dtt[TP, :], dend[TP, :], ALU.mult)
                        P.cp(Btok[TP, :], pb[TP, 0:256], eng="scalar")
                        P.cp(cbT[TP, 0:2 * nt], pb[TP, 256:256 + 2 * nt], eng="scalar")
                        for hf in range(2):
                            src = r8(bA, 0, 512)
                            P.tt(r8(xdt, hf * 512, (hf + 1) * 512), src, b8(dtt, 8 * hf), ALU.mult)
                            P.tt(r8(xdw, hf * 512, (hf + 1) * 512), src, b8(wdt, 8 * hf), ALU.mult)
                            P.tt(r8(xD, hf * 512, (hf + 1) * 512), src, b8(pvl, PV_DSK + 8 * hf), ALU.mult)
                            if hf == 0:
                                for c in range(4, 8):
                                    P.mm(bA[TP, (c - 4) * 128:(c - 3) * 128], AX_(xc_ap[:, c, jsl]), identb)
                                yield
                        yield
                        for hq in range(4):
                            g = hq // 2
                            sg = bC if hq % 2 == 0 else bB
                            sg3 = sg.v(sg.h[TP, 0:4 * nt].rearrange("p (a b) -> p a b", a=4))
                            P.mm(sg3, identn,
                                 cb.v(cb.h[TP, CB_NEG4:CB_NEG4 + 512].rearrange("p (a b) -> p a b", a=4)[:, :, 0:nt]),
                                 start=True, stop=False)
                            for hh in range(4):
                                h = 4 * hq + hh
                                P.mm(sg[TP, hh * nt:(hh + 1) * nt], ahi.v(ahi.h[TP, h:h + 1].broadcast_to([nt, nt])),
                                     trin, start=False, stop=False)
                                P.mm(sg[TP, hh * nt:(hh + 1) * nt], alo.v(alo.h[TP, h:h + 1].broadcast_to([nt, nt])),
                                     trin, start=False, stop=(hh == 3))
                            yield
                            dTq = dTt[hq % 2]
                            for hh in range(4):
                                h = 4 * hq + hh
                                P.act(dTq[TP, hh * nt:(hh + 1) * nt], sg[TP, hh * nt:(hh + 1) * nt], AF.Exp,
                                      bias=nacum[TP, h:h + 1])
                            P.tt(MT.v(MT.h[TP, 4 * hq:4 * hq + 4, 0:nt]),
                                 dTq.v(dTq.h[TP, 0:4 * nt].rearrange("p (a b) -> p a b", a=4)),
                                 cbT.v(cbT.h[TP, g * nt:(g + 1) * nt].unsqueeze(1).broadcast_to([nt, 4, nt])),
                                 ALU.mult, eng="gpsimd")
                        yield
                        for zi in range(2):
                            zb = bA if zi == 0 else bC
                            _, zwb, zwv = zpc[zi]
                            for k in range(8):
                                P.mm(zb[TP, :], uT[:, k, jsl], zwb.v(zwv[:, k, :]), start=(k == 0), stop=(k == 7))
                            P.act(zs[TP, zi * 512:(zi + 1) * 512], zb[TP, :], AF.Silu, scale=rstok[TP, j:j + 1])
                        yield
                        while F["h"] != j:
                            yield
                        if is_s:
                            for hf in range(2):
                                P.dma("sync", v3(stg, 0, 512, 4),
                                      ssd.v(ssd.h[l, b, hf * 512:(hf + 1) * 512, :].rearrange("(c p) n -> p c n", p=128)))
                                for c in range(4):
                                    P.tr(bB[:, c * 128:(c + 1) * 128], stg[:, c * 128:(c + 1) * 128], identf)
                                P.cp(hT[l][:, hf * 512:(hf + 1) * 512], bB[:])
                                P.cp(hTb[l][:, hf * 512:(hf + 1) * 512], bB[:], eng="scalar")
                        for g in range(2):
                            gs = slice(g * 512, (g + 1) * 512)
                            yi, yo, stp = bA, bB, bC
                            for hh in range(8):
                                h = 8 * g + hh
                                P.mm(yi[TP, hh * 64:(hh + 1) * 64], MT[TP, h, 0:nt], xdt[TP, h * 64:(h + 1) * 64],
                                     start=(hh == 0), stop=False, skip_group_check=True)
                            P.mm(yi[TP, :], identn, xD[TP, gs], start=False, stop=True, skip_group_check=True)
                            P.mm(yo[TP, :], AX_(xc_ap[:, 10 + g, jsl]), hTb[l][:, gs])
                            P.mm(stp[:], Btok[TP, g * 128:(g + 1) * 128], xdw[TP, gs])
                            yield
                            yt = ytmp[g]
                            P.tt(r8(yt, 0, 512), r8(yo, 0, 512), b8(eacum, 8 * g), ALU.mult)
                            P.tt(yt[TP, :], yi[TP, :], yt[TP, :], ALU.add)
                            hv = hT[l].v(hT[l].h[:, gs].rearrange("p (a b) -> p a b", a=8))
                            P.tt(hv, hv, cdec.v(cdec.h[:, 8 * g:8 * g + 8].unsqueeze(2).broadcast_to([128, 8, 64])),
                                 ALU.mult, eng="gpsimd")
                            P.tt(hT[l][:, gs], hT[l][:, gs], stp[:], ALU.add)
                            P.cp(hTb[l][:, gs], hT[l][:, gs], eng="scalar")
                            P.tt(yt[TP, :], yt[TP, :], zs[TP, gs], ALU.mult)
                            P.act(sqj[TP, :], yt[TP, :], AF.Square)
                            P.op("vector", "tensor_reduce", [sqj[TP, :]], [ss2[TP, g:g + 1]], out=ss2[TP, g:g + 1],
                                 in_=sqj[TP, :], axis=AX.X, op=ALU.add)
                            yield
                        if emits(j):
                            for hf in range(2):
                                for c in range(4):
                                    P.tr(bC[:, c * 128:(c + 1) * 128], hT[l][:, (hf * 4 + c) * 128:(hf * 4 + c + 1) * 128],
                                         identf)
                                P.cp(stg[:], bC[:], eng=("scalar" if hf else "vector"))
                                dsth = o_sh.h[l, b] if is_s else o_ph.h[l]
                                dstT = o_sh if is_s else o_ph
                                P.dma("sync", dstT.v(dsth[hf * 512:(hf + 1) * 512, :].rearrange("(c p) n -> p c n", p=128)),
                                      v3(stg, 0, 512, 4))
                        F["h"] = j + 1
                        yield
                        P.act(ss2[TP, :], ss2[TP, :], AF.Ln, scale=1.0 / 512, bias=cf[TP, 128:129])
                        P.act(ss2[TP, :], ss2[TP, :], AF.Exp, scale=-0.5)
                        for g in range(2):
                            P.act(ysm[TP, g * 512:(g + 1) * 512], ytmp[g][TP, :], AF.Copy, scale=ss2[TP, g:g + 1])
                        pxs = [bA, bB]
                        for c in range(8):
                            P.mm(pxs[c // 4][:, (c % 4) * nt:(c % 4 + 1) * nt], ysm[TP, c * 128:(c + 1) * 128], identn)
                        yield
                        for c in range(8):
                            P.act(mixT[:, 4 + c, jsl], pxs[c // 4][:, (c % 4) * nt:(c % 4 + 1) * nt], AF.Copy,
                                  scale=pvl[:, PV_SSMN + c:PV_SSMN + c + 1])
                        yield
                    for zi in range(2):
                        wrel(zpc[zi][0])
                    rel(bk)

                run([qk_thread([0, 2, 4], qksets[0]), qk_thread([1, 3], qksets[1]), bulk_thread(),
                     att_thread(), ssd_thread(list(range(0, NSB, 2)), ssd_sets[0]),
                     ssd_thread(list(range(1, NSB, 2)), ssd_sets[1])])
                assert len(banks_free) == 8

                def ffn_thread():
                    allb = list(pA) + list(pO) + list(pX)
                    ri = [0]

                    def nb():
                        ri[0] += 1
                        return allb[ri[0] % 8]
                    pj = None
                    for m in range(8):
                        if m % 2 == 0:
                            if pj is not None:
                                wrel(pj)
                            pj, wb, wv = yield from wget(t, l, "O%d" % (m // 2))
                        ps = nb()
                        for c in range(12):
                            P.mm(ps[:, W], wb.v(wv[:, c, (m % 2) * 128:(m % 2 + 1) * 128]), mixT[:, c, W],
                                 start=(c == 0), stop=(c == 11))
                        P.tt(xTk[m][:, W], xTk[m][:, W], ps[:, W], ALU.add)
                    wrel(pj)
                    rmsnorm(PV_GFFN, l, False)
                    for jg in range(6):
                        gj, wgb, wg = yield from wget(t, l, "G%d" % jg)
                        uj, wub, wu = yield from wget(t, l, "U%d" % jg)
                        for ff in range(4 if jg < 5 else 2):
                            fc = jg * 4 + ff
                            fsl = slice(ff * 128, (ff + 1) * 128)
                            pg = nb()
                            for k in range(8):
                                P.mm(pg[:, W], wgb.v(wg[:, k, fsl]), uT[:, k, W], start=(k == 0), stop=(k == 7))
                            pu = nb()
                            for k in range(8):
                                P.mm(pu[:, W], wub.v(wu[:, k, fsl]), uT[:, k, W], start=(k == 0), stop=(k == 7))
                            ts_ = t2 if fc % 2 == 0 else t2b
                            tu_ = t1 if fc % 2 == 0 else t1b
                            P.tt(ts_[:, W], pg[:, W], rs[:, W], ALU.mult)
                            P.tt(tu_[:, W], pu[:, W], rs[:, W], ALU.mult, eng="vector")
                            P.act(ts_[:, W], ts_[:, W], AF.Silu)
                            P.tt(A(act_ap[:, fc, W]), ts_[:, W], tu_[:, W], ALU.mult, xw=[xc_dummy])
                        wrel(gj)
                        wrel(uj)
                    if not (t == NT - 1 and l == NL - 1):
                        build_diagw((l + 1) % NL)
                    for m in range(8):
                        pj, wb, wv = yield from wget(t, l, "N%d" % m)
                        ps = nb()
                        for fc in range(22):
                            P.mm(ps[:, W], wb.v(wv[:, fc, :]), A(act_ap[:, fc, W]), start=(fc == 0), stop=(fc == 21),
                                 xr=[xc_dummy])
                        P.tt(xTk[m][:, W], xTk[m][:, W], ps[:, W], ALU.add)
                        wrel(pj)
                        if l == NL - 1:
                            P.dma("sync", yout[m * 128:(m + 1) * 128, tok0:tok0 + TW], xTk[m][:, W])
                run([ffn_thread()])
        P.emit()
    return nc


def _consts():
    bf = ml_dtypes.bfloat16
    i = np.arange(128)
    ident = np.eye(128, dtype=np.float32)
    tri = (i[:, None] <= i[None, :]).astype(np.float32)
    ones = np.ones((128, 128), np.float32)
    neg = np.where(i[:, None] > i[None, :], -30000.0, 0.0).astype(np.float32)
    neg4 = np.tile(neg, (1, 4))
    mprev = (i[:, None] > i[None, :]).astype(np.float32)
    mcur = (i[:, None] <= i[None, :]).astype(np.float32)
    blk = ((i[:, None] // 64) == (i[None, :] // 64)).astype(np.float32)
    rot = np.zeros((128, 128), np.float32)
    for m in range(128):
        h, d = m // 64, m % 64
        if d < 32:
            rot[h * 64 + d + 32, m] = -1.0
        else:
            rot[h * 64 + d - 32, m] = 1.0
    negp4 = np.tile(np.where(i[:, None] <= i[None, :], -30000.0, 0.0).astype(np.float32), (1, 4))
    cbf = np.concatenate([ident, tri, ones, neg4, mprev, mcur, blk, rot, negp4], axis=1).astype(bf)
    assert cbf.shape[1] == NCB
    cf = np.zeros((128, NCF), np.float32)
    cf[:, 0:128] = ident
    cf[:, 128] = EPS
    cf[:, 129] = 1.0
    cf[:4, 130] = 1.0
    return cbf, cf


def _rope_tables(pos):
    half = HD // 2
    inv = (np.float32(10000.0) ** (-np.arange(half, dtype=np.float32) / np.float32(half))).astype(np.float32)
    ang = pos.astype(np.float32)[None, :] * inv[:, None]
    cos = np.cos(ang).astype(np.float32)
    sin = np.sin(ang).astype(np.float32)
    idx = (np.arange(128) % 64) % 32
    return np.ascontiguousarray(cos[idx]), np.ascontiguousarray(sin[idx])


def _pvec(inp, l):
    pvv = np.zeros((128, NPV), np.float32)
    p = np.arange(128)
    pvv[:, PV_GMIX:PV_GMIX + 8] = inp["norm_mix"][l].reshape(8, 128).T
    pvv[:, PV_GFFN:PV_GFFN + 8] = inp["norm_ffn"][l].reshape(8, 128).T
    pvv[:, PV_QG] = inp["q_norm"][l][p % 64]
    pvv[:, PV_KG] = inp["k_norm"][l][p % 64]
    pvv[:, PV_SINK:PV_SINK + 8] = inp["sinks"][l][None, :]
    cw = inp["conv_w"][l]
    pvv[:, PV_CW:PV_CW + 48] = cw.reshape(4, 12, 128).transpose(2, 1, 0).reshape(128, 48)
    pvv[:, PV_CB:PV_CB + 12] = inp["conv_b"][l].reshape(12, 128).T
    pvv[:, PV_DTB:PV_DTB + 16] = inp["dt_bias"][l][None, :]
    pvv[:, PV_ALOG:PV_ALOG + 16] = inp["a_log"][l][None, :]
    pvv[:, PV_DSK:PV_DSK + 16] = inp["d_skip"][l][None, :]
    pvv[:, PV_SSMN:PV_SSMN + 8] = inp["ssm_norm"][l].reshape(8, 128).T
    return pvv


_NC_CACHE = {}


def run_kernel(inp, n_cores):
    inp = {k: np.asarray(v) for k, v in inp.items()}
    xp, xs = inp["x_prompt"], inp["x_sample"]
    BP, SEQ_, _ = xp.shape
    BS, TS, _ = xs.shape
    assert SEQ_ % 512 == 0 and TS == 4 and BS % (4 * n_cores) == 0
    NPT = SEQ_ // 512
    NSEQ = BS // n_cores
    NST = 1
    NTOK = NPT * 512 + NSEQ * 4
    key = (NPT, NST, NSEQ)
    if key not in _NC_CACHE:
        _NC_CACHE[key] = build(NPT, NST, NSEQ)
    nc = _NC_CACHE[key]
    cbf, cf = _consts()
    pos = np.zeros(NTOK, np.float32)
    pos[:SEQ_] = np.arange(SEQ_)
    for s in range(NSEQ):
        pos[SEQ_ + s * 4:SEQ_ + s * 4 + 4] = PAST + np.arange(4)
    cosT, sinT = _rope_tables(pos)
    pvec = np.stack([_pvec(inp, l) for l in range(NL)])
    shared = {"cosT": cosT, "sinT": sinT, "pvec": pvec, "cbf": cbf, "cf32": cf,
              "w_in": np.ascontiguousarray(inp["w_in"]), "w_out": np.ascontiguousarray(inp["w_out"]),
              "w_gu": np.ascontiguousarray(inp["w_gate_up"]), "w_dn": np.ascontiguousarray(inp["w_down"])}
    in_maps = []
    for c in range(n_cores):
        sq_ = c % BP
        b0 = c * NSEQ
        xT = np.zeros((D, NTOK), np.float32)
        xT[:, :SEQ_] = xp[sq_].T
        xT[:, SEQ_:] = xs[b0:b0 + NSEQ].reshape(NSEQ * 4, D).T
        m = dict(shared)
        m["xT_in"] = xT
        m["cache_k"] = np.ascontiguousarray(inp["cache_win_k"][:, b0:b0 + NSEQ]).reshape(NL, NSEQ, 128, 128)
        m["cache_v"] = np.ascontiguousarray(inp["cache_win_v"][:, b0:b0 + NSEQ]).reshape(NL, NSEQ, 128, 128)
        m["st_conv"] = np.ascontiguousarray(inp["state_conv"][:, b0:b0 + NSEQ])
        m["st_ssm"] = np.ascontiguousarray(inp["state_ssm"][:, b0:b0 + NSEQ]).reshape(NL, NSEQ, 1024, 128)
        in_maps.append(m)
    res = run_bass_kernel_spmd(nc, in_maps, core_ids=list(range(n_cores)))
    R = res.results
    y_p = np.stack([R[s]["yT"][:, :SEQ_].T for s in range(BP)])
    y_s = np.concatenate([R[c]["yT"][:, SEQ_:].reshape(D, NSEQ, 4).transpose(1, 2, 0)
                          for c in range(n_cores)], axis=0)
    pk = np.stack([R[s]["o_pk"] for s in range(BP)], axis=1).reshape(NL, BP, 128, 2, 64)
    pvv = np.stack([R[s]["o_pv"] for s in range(BP)], axis=1).reshape(NL, BP, 128, 2, 64)
    pcv = np.stack([R[s]["o_pc"] for s in range(BP)], axis=1)
    ph = np.stack([R[s]["o_ph"] for s in range(BP)], axis=1).reshape(NL, BP, 16, 64, 128)
    sk = np.concatenate([R[c]["o_sk"] for c in range(n_cores)], axis=1).reshape(NL, BS, 128, 2, 64)
    sv = np.concatenate([R[c]["o_sv"] for c in range(n_cores)], axis=1).reshape(NL, BS, 128, 2, 64)
    sc = np.concatenate([R[c]["o_sc"] for c in range(n_cores)], axis=1)
    sh = np.concatenate([R[c]["o_sh"] for c in range(n_cores)], axis=1).reshape(NL, BS, 16, 64, 128)
    outs = (y_p, y_s, pk, pvv, pcv, ph, sk, sv, sc, sh)
    return tuple(np.ascontiguousarray(o, dtype=np.float32) for o in outs)


def kernel(**inputs):
    return run_kernel(inputs, 8)
```

```python
import contextlib
import numpy as np
import ml_dtypes
import concourse.bass as bass
import concourse.mybir as mybir
from concourse.bass_utils import run_bass_kernel_spmd

F32 = mybir.dt.float32
BF16 = mybir.dt.bfloat16
AF = mybir.ActivationFunctionType
ALU = mybir.AluOpType
AX = mybir.AxisListType
ENGS = ["tensor", "vector", "scalar", "gpsimd", "sync"]


class V:
    __slots__ = ("t", "ap")

    def __init__(self, t, ap):
        self.t = t
        self.ap = ap


class T:
    def __init__(self, h, name, space):
        self.h = h
        self.name = name
        self.space = space
        self.w = {}
        self.r = {}
        self.dcnt = 0

    def __getitem__(self, idx):
        return V(self, self.h[idx])

    def v(self, ap):
        return V(self, ap)


class Prog:
    def __init__(self, nc, stack):
        self.nc = nc
        self.stack = stack
        self.q = {e: [] for e in ENGS}
        self.cnt = {e: 0 for e in ENGS}
        self.known = {e: {} for e in ENGS}
        self.sems = {}
        self.nsem = 0
        self.dma_owners = []

    def sbuf(self, name, shape, dt):
        h = self.stack.enter_context(self.nc.sbuf_tensor(name, list(shape), dt))
        return T(h, name, "sbuf")

    def psum(self, name, shape, dt):
        h = self.stack.enter_context(self.nc.psum_tensor(name, list(shape), dt))
        return T(h, name, "psum")

    def dram(self, name, shape, dt, kind):
        h = self.nc.dram_tensor(name, list(shape), dt, kind=kind).ap()
        return T(h, name, "dram")

    def semh(self, key):
        if key not in self.sems:
            self.nsem += 1
            nm = key if isinstance(key, str) else "d_" + key.name
            self.sems[key] = self.stack.enter_context(self.nc.semaphore("s_" + nm))
        return self.sems[key]

    def _wait(self, eng, key, val):
        if val <= 0 or self.known[eng].get(key, 0) >= val:
            return
        self.known[eng][key] = val
        sem = self.semh(key)
        self.q[eng].append(lambda e, sem=sem, val=val: e.wait_ge(sem, val))

    def _deps(self, eng, reads, writes, skipw=None):
        for v in reads:
            for key, val in v.t.w.items():
                if key == eng and eng == "tensor":
                    continue
                self._wait(eng, key, val)
            if v.t.space == "psum":
                for key, val in v.t.r.items():
                    if key != eng:
                        self._wait(eng, key, val)
        for v in writes:
            for key, val in v.t.w.items():
                if (key == eng and eng == "tensor") or key is skipw:
                    continue
                self._wait(eng, key, val)
            for key, val in v.t.r.items():
                if key == eng and eng == "tensor":
                    continue
                self._wait(eng, key, val)

    def op(self, eng, meth, reads, writes, *args, **kw):
        reads = list(reads) + list(kw.pop("xr", []))
        writes = list(writes) + list(kw.pop("xw", []))
        self._deps(eng, reads, writes)
        self.cnt[eng] += 1
        c = self.cnt[eng]
        for v in reads:
            v.t.r[eng] = c
        for v in writes:
            v.t.w[eng] = c
        sem = self.semh(eng)
        kw2 = {k: (x.ap if isinstance(x, V) else x) for k, x in kw.items()}
        self.q[eng].append(
            lambda e, meth=meth, kw2=kw2, sem=sem: getattr(e, meth)(**kw2).then_inc(sem, 1)
        )

    def dma(self, eng, out, in_, owner=None, **kw):
        if owner is None:
            owner = out.t if out.t.space == "sbuf" else in_.t
        self._deps(eng, [in_], [out], skipw=owner)
        if owner.dcnt == 0:
            self.dma_owners.append(owner)
        owner.dcnt += 16
        in_.t.r[owner] = owner.dcnt
        out.t.w[owner] = owner.dcnt
        sem = self.semh(owner)
        oa, ia = out.ap, in_.ap
        self.q[eng].append(
            lambda e, oa=oa, ia=ia, sem=sem, kw=kw: e.dma_start(out=oa, in_=ia, **kw).then_inc(sem, 16)
        )

    def cnt_total(self):
        return sum(self.cnt.values()) + sum(o.dcnt for o in self.dma_owners) + len(self.q["sync"])

    def finish(self):
        for o in self.dma_owners:
            self._wait("sync", o, o.dcnt)
        for e in ENGS:
            if e != "sync" and self.cnt[e] > 0:
                self._wait("sync", e, self.cnt[e])

    def emit(self):
        self.finish()
        with self.nc.Block() as block:
            for e in ENGS:
                ql = self.q[e]
                if not ql:
                    continue

                def section(eng, ql=ql):
                    for f in ql:
                        f(eng)

                getattr(block, e)(section)

    def mm(self, out, lhsT, rhs, start=True, stop=True, **kw):
        self.op("tensor", "matmul", [lhsT, rhs], [out], out=out, lhsT=lhsT, rhs=rhs,
                start=start, stop=stop, **kw)

    def tr(self, out, in_, ident):
        self.op("tensor", "transpose", [in_, ident], [out], out=out, in_=in_, identity=ident)

    def act(self, out, in_, func, bias=None, scale=None, accum_out=None, eng="scalar"):
        reads = [in_]
        kw = dict(out=out, in_=in_, func=func)
        if bias is not None:
            kw["bias"] = bias
            if isinstance(bias, V):
                reads.append(bias)
        if scale is not None:
            kw["scale"] = scale
            if isinstance(scale, V):
                reads.append(scale)
        writes = [out]
        if accum_out is not None:
            kw["accum_out"] = accum_out
            writes.append(accum_out)
        self.op("scalar", "activation", reads, writes, **kw)

    def tt(self, out, in0, in1, op, eng="vector", **kw):
        self.op(eng, "tensor_tensor", [in0, in1], [out], out=out, in0=in0, in1=in1, op=op, **kw)

    def ts(self, out, in0, s1, op0, s2=None, op1=None, eng="vector"):
        reads = [in0] + [x for x in (s1, s2) if isinstance(x, V)]
        kw = dict(out=out, in0=in0, scalar1=s1, scalar2=s2, op0=op0)
        if op1 is not None:
            kw["op1"] = op1
        self.op(eng, "tensor_scalar", reads, [out], **kw)

    def stt(self, out, in0, scalar, in1, op0, op1):
        reads = [in0, in1] + ([scalar] if isinstance(scalar, V) else [])
        self.op("vector", "scalar_tensor_tensor", reads, [out], out=out, in0=in0, scalar=scalar,
                in1=in1, op0=op0, op1=op1)

    def cp(self, out, in_, eng="vector"):
        if eng == "scalar":
            self.act(out, in_, AF.Copy)
        else:
            self.op(eng, "tensor_copy", [in_], [out], out=out, in_=in_)


D = 1024
HD = 64
NH = 16
CONVD = 1536
DFF = 2816
INW = 3344
NL = 2
EPS = 1e-6
ATT_SCALE = HD ** -0.5
PAST = 16384

PV_GMIX, PV_GFFN, PV_QG, PV_KG, PV_SINK, PV_CW, PV_CB, PV_DTB, PV_ALOG, PV_DSK, PV_SSMN = (
    0, 8, 16, 17, 18, 26, 74, 86, 102, 118, 134)
NPV = 142
CB_ID, CB_TRI, CB_ONES, CB_NEG4, CB_MPREV, CB_MCUR, CB_BLK, CB_ROT, CB_NEGP4 = 0, 128, 256, 384, 896, 1024, 1152, 1280, 1408
NCB = 1920
NCF = 131


def in_pieces():
    return [("X0", 1792, 512), ("X1", 2304, 512), ("X2", 2816, 512), ("DT", 3328, 16),
            ("Q", 0, 512), ("KV", 512, 256), ("Z0", 768, 512), ("Z1", 1280, 512)]


def layer_pieces():
    pcs = [(n, "w_in", 8, c0, nc_) for (n, c0, nc_) in in_pieces()]
    pcs += [("O%d" % m, "w_out", 12, m * 256, 256) for m in range(4)]
    for j in range(6):
        c0 = j * 512
        w = min(512, DFF - c0)
        pcs.append(("G%d" % j, "w_gu", 8, c0, w))
        pcs.append(("U%d" % j, "w_gu", 8, DFF + c0, w))
    pcs += [("N%d" % m, "w_dn", 22, m * 128, 128) for m in range(8)]
    return pcs


def build(NPT, NST, NSEQ):
    nc = bass.Bass("TRN2", target_bir_lowering=False)
    NT = NPT + NST
    NTOK = NPT * 512 + NSEQ * 4
    NSBMAX = max(4, NSEQ)
    assert NST == 1
    with contextlib.ExitStack() as st:
        P = Prog(nc, st)
        xin = P.dram("xT_in", [D, NTOK], F32, "ExternalInput")
        cosd = P.dram("cosT", [128, NTOK], F32, "ExternalInput")
        sind = P.dram("sinT", [128, NTOK], F32, "ExternalInput")
        ckd = P.dram("cache_k", [NL, NSEQ, 128, 128], F32, "ExternalInput")
        cvd = P.dram("cache_v", [NL, NSEQ, 128, 128], F32, "ExternalInput")
        scd = P.dram("st_conv", [NL, NSEQ, 3, CONVD], F32, "ExternalInput")
        ssd = P.dram("st_ssm", [NL, NSEQ, 1024, 128], F32, "ExternalInput")
        wd = {"w_in": P.dram("w_in", [NL, D, INW], F32, "ExternalInput"),
              "w_out": P.dram("w_out", [NL, CONVD, D], F32, "ExternalInput"),
              "w_gu": P.dram("w_gu", [NL, D, 2 * DFF], F32, "ExternalInput"),
              "w_dn": P.dram("w_dn", [NL, DFF, D], F32, "ExternalInput")}
        pvd = P.dram("pvec", [NL, 128, NPV], F32, "ExternalInput")
        cbd = P.dram("cbf", [128, NCB], BF16, "ExternalInput")
        cfd = P.dram("cf32", [128, NCF], F32, "ExternalInput")
        yout = P.dram("yT", [D, NTOK], F32, "ExternalOutput")
        o_pk = P.dram("o_pk", [NL, 128, 128], F32, "ExternalOutput")
        o_pv = P.dram("o_pv", [NL, 128, 128], F32, "ExternalOutput")
        o_pc = P.dram("o_pc", [NL, 3, CONVD], F32, "ExternalOutput")
        o_ph = P.dram("o_ph", [NL, 1024, 128], F32, "ExternalOutput")
        o_sk = P.dram("o_sk", [NL, NSEQ, 128, 128], F32, "ExternalOutput")
        o_sv = P.dram("o_sv", [NL, NSEQ, 128, 128], F32, "ExternalOutput")
        o_sc = P.dram("o_sc", [NL, NSEQ, 3, CONVD], F32, "ExternalOutput")
        o_sh = P.dram("o_sh", [NL, NSEQ, 1024, 128], F32, "ExternalOutput")

        pcs = layer_pieces()
        offs = []
        tot = 0
        for (_, _, nk, _, ncol) in pcs:
            offs.append(tot)
            tot += nk * ncol
        wscr = [P.dram("wscr%d" % l, [128, tot], BF16, "Internal") for l in range(NL)]

        xTk = [P.sbuf("xT%d" % k, [128, 512], F32) for k in range(8)]
        uT = P.sbuf("uT", [128, 8, 512], BF16)
        sq = [P.sbuf("sq%d" % i, [128, 512], BF16) for i in range(2)]
        rs = P.sbuf("rs", [128, 512], F32)
        rstok = P.sbuf("rstok", [128, 16], F32)
        t1 = P.sbuf("t1", [128, 512], F32)
        t2 = P.sbuf("t2", [128, 512], F32)
        qn = P.sbuf("qn", [128, 512], BF16)
        t1b = P.sbuf("t1b", [128, 512], F32)
        t2b = P.sbuf("t2b", [128, 512], F32)
        qnb = P.sbuf("qnb", [128, 512], BF16)
        qr = P.sbuf("qr", [128, 4, 512], BF16)
        krc = P.sbuf("krc", [128, 4, 128], BF16)
        krp = P.sbuf("krp", [128, NSBMAX, 128], BF16)
        vcur = P.sbuf("vcur", [128, NSBMAX, 2, 65], BF16)
        vprev = P.sbuf("vprev", [128, NSBMAX, 2, 65], BF16)
        carry_k = [P.sbuf("carry_k%d" % l, [128, 128], BF16) for l in range(NL)]
        carry_v = [P.sbuf("carry_v%d" % l, [128, 2, 65], BF16) for l in range(NL)]
        ctail = [P.sbuf("ctail%d" % l, [128, 12, 3], BF16) for l in range(NL)]
        hT = [P.sbuf("hT%d" % l, [128, 1024], F32) for l in range(NL)]
        hTb = [P.sbuf("hTb%d" % l, [128, 1024], BF16) for l in range(NL)]
        dtraw = P.sbuf("dtraw", [128, NSBMAX, 16], F32)
        arena = P.sbuf("arena", [128, 12432], BF16)
        mixT = P.sbuf("mixT", [128, 12, 512], BF16)
        wbuf = [P.sbuf("wbuf%d" % i, [128, 4096], BF16) for i in range(3)]
        diagw = P.sbuf("diagw", [128, 48, 128], BF16)
        cs = P.sbuf("cs", [128, 512], F32)
        sn = P.sbuf("sn", [128, 512], F32)
        cb = P.sbuf("cb", [128, NCB], BF16)
        cf = P.sbuf("cf", [128, NCF], F32)
        pv = [P.sbuf("pv%d" % l, [128, NPV], F32) for l in range(NL)]
        esink = [P.sbuf("esink%d" % l, [128, 8], F32) for l in range(NL)]
        Aneg = [P.sbuf("Aneg%d" % l, [128, 16], F32) for l in range(NL)]
        pT = [P.sbuf("pTs%d" % i, [128, 512], BF16) for i in range(2)]
        on = P.sbuf("on", [128, 512], BF16)
        den = P.sbuf("den", [128, 8], F32)
        def ssd_set(i):
            d = {}
            for nm in ("e_", "dtt", "a_", "nacum", "eacum", "dend", "wdt", "cdec"):
                d[nm] = P.sbuf("%s%d" % (nm, i), [128, 16], F32)
            d["ahi"] = P.sbuf("ahi%d" % i, [128, 16], BF16)
            d["alo"] = P.sbuf("alo%d" % i, [128, 16], BF16)
            d["xdt"] = P.sbuf("xdt%d" % i, [128, 1024], BF16)
            d["xdw"] = P.sbuf("xdw%d" % i, [128, 1024], BF16)
            d["xD"] = P.sbuf("xD%d" % i, [128, 1024], BF16)
            d["zs"] = P.sbuf("zs%d" % i, [128, 1024], BF16)
            d["Btok"] = P.sbuf("Btok%d" % i, [128, 256], BF16)
            d["cbT"] = P.sbuf("cbT%d" % i, [128, 256], F32)
            d["dT"] = P.sbuf("dT%d" % i, [128, 512], F32)
            d["MT"] = P.sbuf("MT%d" % i, [128, 16, 128], BF16)
            d["ytmp"] = [P.sbuf("ytmp%d_%d" % (i, g), [128, 512], F32) for g in range(2)]
            d["ysm"] = P.sbuf("ysm%d" % i, [128, 1024], BF16)
            d["sqj"] = P.sbuf("sqj%d" % i, [128, 512], BF16)
            d["ss2"] = P.sbuf("ss2%d" % i, [128, 2], F32)
            return d
        ssd_sets = [ssd_set(0), ssd_set(1)]
        stg = P.sbuf("stg", [128, 1024], F32)
        stc = P.sbuf("stc", [3, 512], F32)

        class _Alias:
            def __init__(self, t, c0):
                self.t, self.c0 = t, c0

            def __getitem__(self, idx):
                rows, cols = idx if isinstance(idx, tuple) else (idx, slice(None))
                c0 = self.c0 + (cols.start or 0)
                c1 = self.c0 + (cols.stop if cols.stop is not None else 128)
                return V(self.t, self.t.h[rows, c0:c1])
        vout = _Alias(stg, 512)
        kout = _Alias(stg, 640)
        stail = P.sbuf("stail", [128, 12, NSBMAX, 3], BF16)

        pA = [P.psum("pA%d" % i, [128, 512], F32) for i in range(4)]
        pO = [P.psum("pO%d" % i, [128, 512], F32) for i in range(2)]
        pX = [P.psum("pX%d" % i, [128, 512], F32) for i in range(2)]
        ring = {"a": 0}

        def nextA():
            t = pA[ring["a"] % 4]
            ring["a"] += 1
            return t

        xpre_ap = arena.h[:, 0:6288].rearrange("p (c s t) -> p c s t", c=12, s=4, t=131)
        xc_ap = arena.h[:, 6288:12432].rearrange("p (c t) -> p c t", c=12, t=512)
        act_ap = arena.h[:, 0:11264].rearrange("p (f t) -> p f t", f=22, t=512)
        A = arena.v
        arena_xc = T(arena.h, "arena_xc", "sbuf")
        AX_ = arena_xc.v
        xc_dummy = AX_(xc_ap[:, 0, 0:1])

        def C(col, n=128):
            return cb[:, col:col + n]

        identb = C(CB_ID)
        tri = C(CB_TRI)
        onesb = C(CB_ONES)
        neg4 = C(CB_NEG4, 512)
        mprev = C(CB_MPREV)
        mcur = C(CB_MCUR)
        blk = C(CB_BLK)
        rotm = C(CB_ROT)
        identf = cf[:, 0:128]
        onec = cf[:, 129:130]
        validc = cf[:, 130:131]

        def bc8(t, c0, n=8, d=64):
            return t.v(t.h[:, c0:c0 + n].unsqueeze(2).broadcast_to([128, n, d]))

        def v3(t, c0, c1, a):
            return t.v(t.h[:, c0:c1].rearrange("p (a b) -> p a b", a=a))

        def memset(tv, val):
            P.op("vector", "memset", [], [tv], ap=tv, constant=val)

        P.dma("sync", cb[:], cbd[:, :])
        P.dma("sync", cf[:], cfd[:, :])
        for l in range(NL):
            P.dma("sync", pv[l][:], pvd[l, :, :])
        for l in range(NL):
            for i, (_, wn, nk, c0, ncol) in enumerate(pcs):
                src = wd[wn].h[l, :, c0:c0 + ncol].rearrange("(k p) c -> p k c", p=128)
                dst = wscr[l].h[:, offs[i]:offs[i] + nk * ncol].rearrange("p (k c) -> p k c", k=nk)
                if pcs[i][0] == "Q":
                    s5 = src.rearrange("p k (two four d) -> p k two four d", two=2, four=4)
                    d5 = dst.rearrange("p k (four two d) -> p k two four d", two=2, four=4)
                    for tw in range(2):
                        for fo in range(4):
                            P.dma("gpsimd", wscr[l].v(d5[:, :, tw, fo, :]), wd[wn].v(s5[:, :, tw, fo, :]),
                                  owner=wscr[l])
                    continue
                P.dma("gpsimd", wscr[l].v(dst), wd[wn].v(src), owner=wscr[l])
        for l in range(NL):
            P.act(esink[l][:], pv[l][:, PV_SINK:PV_SINK + 8], AF.Exp)
            P.act(Aneg[l][:], pv[l][:, PV_ALOG:PV_ALOG + 16], AF.Exp)
            P.ts(Aneg[l][:], Aneg[l][:], -1.0, ALU.mult)
            memset(hT[l][:], 0.0)
            memset(hTb[l][:], 0.0)
            memset(ctail[l][:], 0.0)
            memset(carry_k[l][:], 0.0)
            memset(carry_v[l][:], 1.0)
        memset(vcur[:], 1.0)
        memset(vprev[:], 1.0)

        def build_diagw(lq):
            for c in range(12):
                for i in range(4):
                    col = PV_CW + c * 4 + i
                    P.act(diagw[:, c * 4 + i, :], identb, AF.Copy, scale=pv[lq][:, col:col + 1])
        build_diagw(0)

        banks_free = list(pA) + list(pO) + list(pX)
        dbgF = [None]

        def acq(n):
            while len(banks_free) < n:
                yield
            got = [banks_free.pop(0) for _ in range(n)]
            return got

        def rel(bs):
            banks_free.extend(bs)

        def run(tasks):
            active = list(tasks)
            idle_rounds = 0
            while active:
                progressed = False
                for g in list(active):
                    before = P.cnt_total()
                    try:
                        next(g)
                    except StopIteration:
                        active.remove(g)
                        progressed = True
                        continue
                    if P.cnt_total() != before:
                        progressed = True
                idle_rounds = 0 if progressed else idle_rounds + 1
                if idle_rounds > 64:
                    raise RuntimeError("emission scheduler deadlock: tasks=%s banks_free=%d F=%s wst=%s occ=%s refc=%s" % (
                        [g.__name__ for g in active], len(banks_free), dbgF[0], wst, occ,
                        {k: v for k, v in refc.items() if v > 0}))

        plan = []
        plan_idx = {}
        for t in range(NT):
            for l in range(NL):
                for i, pc_ in enumerate(pcs):
                    plan_idx[(t, l, pc_[0])] = len(plan)
                    plan.append((l, i))
        USERS = {"Q": 2, "KV": 2, "Z0": 2, "Z1": 2}
        wst = {"issued": 0}
        occ = [None] * 3
        refc = {}

        def try_prefetch(upto):
            while wst["issued"] <= min(upto, len(plan) - 1):
                jj = wst["issued"]
                b = jj % 3
                if occ[b] is not None and refc[occ[b]] > 0:
                    return
                ll, ii = plan[jj]
                n = pcs[ii][2] * pcs[ii][4]
                P.dma("sync", wbuf[b][:, 0:n], wscr[ll][:, offs[ii]:offs[ii] + n])
                occ[b] = jj
                refc[jj] = USERS.get(pcs[ii][0], 1)
                wst["issued"] += 1

        def wget(t, l, name):
            j = plan_idx[(t, l, name)]
            while True:
                try_prefetch(j + 2)
                if wst["issued"] > j:
                    break
                yield
            pi = plan[j][1]
            nk, ncol = pcs[pi][2], pcs[pi][4]
            wb = wbuf[j % 3]
            return j, wb, wb.h[:, 0:nk * ncol].rearrange("p (k c) -> p k c", k=nk)

        def wrel(j):
            refc[j] -= 1
            assert refc[j] >= 0

        def rstd_inplace(tv):
            P.act(tv, tv, AF.Ln)
            P.act(tv, tv, AF.Exp, scale=-0.5)

        krcf = krc.h[:].rearrange("p s t -> p (s t)")
        qksets = [(sq[0], t1, t2, qn), (sq[1], t1b, t2b, qnb)]

        presetup_done = {}

        def sample_presetup(ts_, ls_):
            presetup_done[(ts_, ls_)] = True
            for j in range(NSEQ):
                b = (ts_ - NPT) * NSEQ + j
                P.dma("sync", stg[:, 0:128], ckd[ls_, b, :, :])
                P.tr(pX[1][:, 0:128], stg[:, 0:128], identf)
                P.cp(krp[:, j, :], pX[1][:, 0:128])
                P.dma("sync", stg[:, 128:256], cvd[ls_, b, :, :])
                P.cp(vprev.v(vprev.h[:, j, :, 0:64]), v3(stg, 128, 256, 2))
                P.dma("sync", o_sk[ls_, b, 0:124, :], ckd[ls_, b, 4:128, :], owner=o_sk)
                P.dma("sync", o_sv[ls_, b, 0:124, :], cvd[ls_, b, 4:128, :], owner=o_sv)
                tp2 = pA[1]
                for q3 in range(3):
                    P.dma("sync", stc[:], scd[ls_, b, :, q3 * 512:(q3 + 1) * 512])
                    for cc in range(4):
                        c = q3 * 4 + cc
                        P.mm(tp2[:, c * 3:c * 3 + 3], stc[0:3, cc * 128:(cc + 1) * 128], cf[0:3, 0:3])
                P.cp(stail[:, :, j, :], v3(tp2, 0, 36, 12))

        for t in range(NT):
            is_s = t >= NPT
            nt = 4 if is_s else 128
            NSB = NSEQ if is_s else 4
            TW = NSB * nt
            W = slice(0, TW)
            TP = slice(0, nt)
            SBW = 3 + nt
            tok0 = t * 512 if not is_s else NPT * 512 + (t - NPT) * TW
            xpre_ap = arena.h[:, 0:12 * NSB * SBW].rearrange("p (c s t) -> p c s t", c=12, s=NSB, t=SBW)
            for k in range(8):
                P.dma("sync", xTk[k][:, W], xin[k * 128:(k + 1) * 128, tok0:tok0 + TW])
            P.dma("sync", cs[:, W], cosd[:, tok0:tok0 + TW])
            P.dma("sync", sn[:, W], sind[:, tok0:tok0 + TW])

            def rmsnorm(gcol, l, want_tok):
                for k in range(8):
                    P.ts(uT[:, k, W], xTk[k][:, W], pv[l][:, gcol + k:gcol + k + 1], ALU.mult)
                ps = pA[0]
                for k in range(8):
                    s = sq[k % 2]
                    P.act(s[:, W], xTk[k][:, W], AF.Square)
                    P.mm(ps[:, W], onesb, s[:, W], start=(k == 0), stop=(k == 7))
                P.act(rs[:, W], ps[:, W], AF.Ln, scale=1.0 / D, bias=cf[:, 128:129])
                P.act(rs[:, W], rs[:, W], AF.Exp, scale=-0.5)
                if want_tok:
                    pt_ = pA[1]
                    for j in range(NSB):
                        P.mm(pt_[TP, j:j + 1], rs[0:1, j * nt:(j + 1) * nt], cf[0:1, 129:130])
                    P.cp(rstok[TP, 0:NSB], pt_[TP, 0:NSB])

            for l in range(NL):
                pvl = pv[l]
                F = {"qk": 0, "v": False, "conv": False, "dt": False}
                dbgF[0] = F

                def sub_jsl(j):
                    return slice(j * nt, (j + 1) * nt)

                def emits(j):
                    return is_s or (t == NPT - 1 and j == 3)

                def bidx(j):
                    return (t - NPT) * NSB + j

                if is_s:
                    if not presetup_done.get((t, l)):
                        sample_presetup(t, l)
                    P.cp(A(xpre_ap[:, :, :, 0:3]), stail[:, :, 0:NSB, :])
                else:
                    P.cp(krp[:, 0, :], carry_k[l][:])
                    P.cp(vprev[:, 0, :, :], carry_v[l][:])
                    P.cp(A(xpre_ap[:, :, 0, 0:3]), ctail[l][:])

                rmsnorm(PV_GMIX, l, True)

                def qk_thread(chunks, tset):
                    while not F["conv"]:
                        yield
                    bk = None
                    s, t1_, t2_, qn_ = tset
                    for ci, c in enumerate(chunks):
                        isk = (c == 4)
                        pj, wb, wv = yield from wget(t, l, "KV" if isk else "Q")
                        if bk is None:
                            bk = yield from acq(2)
                        ps = bk[0]
                        for k in range(8):
                            lh = wv[:, k, 0:128] if isk else wv[:, k, c * 128:(c + 1) * 128]
                            P.mm(ps[:, W], wb.v(lh), uT[:, k, W], start=(k == 0), stop=(k == 7))
                        if isk or ci == len([x for x in chunks if x < 4]) - 1:
                            wrel(pj)
                        yield
                        qw = t2_
                        P.tt(qw[:, W], ps[:, W], rs[:, W], ALU.mult)
                        P.act(s[:, W], qw[:, W], AF.Square)
                        ms = bk[1]
                        P.mm(ms[:, W], blk, s[:, W])
                        yield
                        P.act(t1_[:, W], ms[:, W], AF.Ln, scale=1.0 / HD, bias=cf[:, 128:129])
                        P.act(t1_[:, W], t1_[:, W], AF.Exp, scale=-0.5)
                        gcol = PV_KG if isk else PV_QG
                        P.stt(qn_[:, W], qw[:, W], pvl[:, gcol:gcol + 1], t1_[:, W], ALU.mult, ALU.mult)
                        rot = bk[1]
                        P.mm(rot[:, W], rotm, qn_[:, W])
                        yield
                        P.tt(t2_[:, W], qn_[:, W], cs[:, W], ALU.mult)
                        P.tt(t1_[:, W], rot[:, W], sn[:, W], ALU.mult)
                        outv = krc.v(krcf[:, W]) if isk else qr[:, c, W]
                        P.tt(outv, t2_[:, W], t1_[:, W], ALU.add)
                        F["qk"] += 1
                        yield
                    rel(bk)

                def bulk_thread():
                    bk = yield from acq(4)
                    ri = [0]

                    def nb():
                        ri[0] += 1
                        return bk[ri[0] % len(bk)]
                    for xi in range(3):
                        pj, wb, wv = yield from wget(t, l, "X%d" % xi)
                        for cc in range(4):
                            c = xi * 4 + cc
                            ps = nb()
                            for k in range(8):
                                P.mm(ps[:, W], wb.v(wv[:, k, cc * 128:(cc + 1) * 128]), uT[:, k, W],
                                     start=(k == 0), stop=(k == 7))
                            P.tt(A(xpre_ap[:, c, :, 3:3 + nt]), ps.v(ps.h[:, W].rearrange("p (a b) -> p a b", a=NSB)),
                                 rs.v(rs.h[:, W].rearrange("p (a b) -> p a b", a=NSB)), ALU.mult)
                            yield
                        wrel(pj)
                    pj, wb, wv = yield from wget(t, l, "DT")
                    for j in range(NSB):
                        ps = nb()
                        for k in range(8):
                            P.mm(ps[TP, 0:16], uT[:, k, sub_jsl(j)], wb.v(wv[:, k, :]), start=(k == 0), stop=(k == 7))
                        P.stt(dtraw[TP, j, :], ps[TP, 0:16], rstok[TP, j:j + 1], pvl[TP, PV_DTB:PV_DTB + 16],
                              ALU.mult, ALU.add)
                    wrel(pj)
                    F["dt"] = True
                    yield
                    if not is_s:
                        P.cp(A(xpre_ap[:, :, 1:4, 0:3]), A(xpre_ap[:, :, 0:3, nt:nt + 3]))
                        P.cp(ctail[l][:], A(xpre_ap[:, :, 3, nt:nt + 3]))
                    for j in range(NSB):
                        if not emits(j):
                            continue
                        for q3 in range(3):
                            pa = nb()
                            for cc in range(4):
                                c = q3 * 4 + cc
                                P.mm(pa[0:3, cc * 128:(cc + 1) * 128], A(xpre_ap[:, c, j, nt:nt + 3]), identb)
                            P.cp(stc[:], pa[0:3, :])
                            if is_s:
                                P.dma("sync", o_sc[l, bidx(j), :, q3 * 512:(q3 + 1) * 512], stc[:])
                            else:
                                P.dma("sync", o_pc[l, :, q3 * 512:(q3 + 1) * 512], stc[:])
                        yield
                    for c in range(12):
                        ps = nb()
                        for i in range(4):
                            P.mm(ps.v(ps.h[:, W].rearrange("p (a b) -> p a b", a=NSB)), diagw[:, c * 4 + i, :],
                                 A(xpre_ap[:, c, :, i:i + nt]), start=(i == 0), stop=(i == 3))
                        P.act(AX_(xc_ap[:, c, W]), ps[:, W], AF.Silu, bias=pvl[:, PV_CB + c:PV_CB + c + 1])
                        yield
                    F["conv"] = True
                    rel(bk[2:])
                    bk = bk[:2]
                    pj, wb, wv = yield from wget(t, l, "KV")
                    for j in range(NSB):
                        ps = nb()
                        for k in range(8):
                            P.mm(ps[TP, 0:128], uT[:, k, sub_jsl(j)], wb.v(wv[:, k, 128:256]),
                                 start=(k == 0), stop=(k == 7))
                        P.act(vcur.v(vcur.h[TP, j, :, 0:64]), ps.v(ps.h[TP, 0:128].rearrange("p (a b) -> p a b", a=2)),
                              AF.Copy, scale=rstok[TP, j:j + 1])
                        if emits(j):
                            P.act(vout[TP, :], ps[TP, 0:128], AF.Copy, scale=rstok[TP, j:j + 1])
                            if is_s:
                                P.dma("sync", o_sv[l, bidx(j), 124:128, :], vout[0:4, :])
                            else:
                                P.dma("sync", o_pv[l, :, :], vout[:])
                        yield
                    wrel(pj)
                    F["v"] = True
                    rel(bk)

                def att_thread():
                    while F["qk"] < 5 or not F["v"]:
                        yield
                    if not is_s:
                        P.cp(krp[:, 1:4, :], krc[:, 0:3, :])
                        P.cp(vprev[:, 1:4, :, :], vcur[:, 0:3, :, :])
                        P.cp(carry_k[l][:], krc[:, 3, :])
                        P.cp(carry_v[l][:], vcur[:, 3, :, :])
                    bk = yield from acq(2)
                    for j in range(NSB):
                        jsl = sub_jsl(j)
                        if emits(j):
                            pa = bk[0]
                            P.mm(pa[TP, 0:128], krc.v(krcf[:, jsl]), identb)
                            P.cp(kout[TP, :], pa[TP, 0:128], eng="scalar")
                            if is_s:
                                P.dma("sync", o_sk[l, bidx(j), 124:128, :], kout[0:4, :])
                            else:
                                P.dma("sync", o_pk[l, :, :], kout[:])
                        for g in range(2):
                            first = True
                            oacc = bk[1]
                            for which in ("prev", "cur"):
                                if which == "prev" and (not is_s) and t == 0 and j == 0:
                                    continue
                                prev = which == "prev"
                                nk_ = 128 if prev else nt
                                KP = slice(0, nk_)
                                S = bk[0]
                                pt = pT[0] if prev else pT[1]
                                klh = krp[64 * g:64 * g + 64, j, :] if prev else krc.v(krcf[64 * g:64 * g + 64, jsl])
                                S3 = S.v(S.h[KP, 0:4 * nt].rearrange("p (a b) -> p a b", a=4))
                                ncol = CB_NEGP4 if prev else CB_NEG4
                                P.mm(S3, cb[KP, CB_ID:CB_ID + nk_],
                                     cb.v(cb.h[KP, ncol:ncol + 512].rearrange("p (a b) -> p a b", a=4)[:, :, 0:nt]),
                                     start=True, stop=False)
                                P.mm(S3, klh, qr.v(qr.h[64 * g:64 * g + 64, :, jsl]), start=False, stop=True)
                                yield
                                P.act(pt[KP, 0:4 * nt], S[KP, 0:4 * nt], AF.Exp, scale=ATT_SCALE)
                                vv = vprev if prev else vcur
                                for h in range(4):
                                    P.mm(oacc[TP, h * 65:(h + 1) * 65], pt[KP, h * nt:(h + 1) * nt],
                                         vv.v(vv.h[KP, j, g, :]), start=(first and h == 0),
                                         stop=(which == "cur" and h == 3), skip_group_check=True)
                                first = False
                                yield
                            o3 = oacc.h[TP, 0:260].rearrange("p (h e) -> p h e", h=4)
                            dg = den[TP, 4 * g:4 * g + 4]
                            P.tt(dg, oacc.v(o3[:, :, 64]), esink[l][TP, 4 * g:4 * g + 4], ALU.add)
                            P.op("vector", "reciprocal", [dg], [dg], out=dg, in_=dg)
                            P.tt(on.v(on.h[TP, g * 256:(g + 1) * 256].rearrange("p (a b) -> p a b", a=4)),
                                 oacc.v(o3[:, :, 0:64]),
                                 den.v(den.h[TP, 4 * g:4 * g + 4].unsqueeze(2).broadcast_to([nt, 4, 64])), ALU.mult)
                            yield
                        tb = bk[0]
                        for c in range(4):
                            P.mm(tb[:, c * nt:(c + 1) * nt], on[TP, c * 128:(c + 1) * 128], cb[TP, CB_ID:CB_ID + nt])
                        P.cp(mixT.v(mixT.h[:, 0:4, jsl]), tb.v(tb.h[:, 0:4 * nt].rearrange("p (a b) -> p a b", a=4)),
                             eng="scalar")
                        yield
                    rel(bk)

                F["h"] = 0

                def ssd_thread(js, S):
                    while not (F["conv"] and F["dt"]):
                        yield
                    bk = yield from acq(3)
                    bA, bB, bC = bk
                    e_, dtt, a_, nacum, eacum, dend, wdt, cdec = (S[k_] for k_ in
                                                                   ("e_", "dtt", "a_", "nacum", "eacum", "dend", "wdt", "cdec"))
                    ahi, alo, xdt, xdw, xD, Btok, cbT, dTt, MT, ytmp, ysm, sqj, ss2, zs = (
                        S[k_] for k_ in ("ahi", "alo", "xdt", "xdw", "xD", "Btok", "cbT", "dT", "MT", "ytmp", "ysm",
                                         "sqj", "ss2", "zs"))
                    zpc = []
                    for zi in range(2):
                        zpc.append((yield from wget(t, l, "Z%d" % zi)))
                    identn = cb[TP, CB_ID:CB_ID + nt]
                    trin = cb[TP, CB_TRI:CB_TRI + nt]

                    def b8(tl, c0):
                        return tl.v(tl.h[TP, c0:c0 + 8].unsqueeze(2).broadcast_to([nt, 8, 64]))

                    def r8(tl, c0, c1):
                        return tl.v(tl.h[TP, c0:c1].rearrange("p (a b) -> p a b", a=8))
                    for j in js:
                        jsl = sub_jsl(j)
                        b = bidx(j) if is_s else None
                        P.act(e_[TP, :], dtraw[TP, j, :], AF.Exp)
                        P.act(dtt[TP, :], e_[TP, :], AF.Ln, bias=cf[TP, 129:130])
                        P.tt(a_[TP, :], dtt[TP, :], Aneg[l][TP, :], ALU.mult)
                        P.cp(ahi[TP, :], a_[TP, :])
                        P.tt(alo[TP, :], a_[TP, :], ahi[TP, :], ALU.subtract)
                        pc = bB
                        P.mm(pc[TP, 0:16], trin, ahi[TP, :], start=True, stop=False)
                        P.mm(pc[TP, 0:16], trin, alo[TP, :], start=False, stop=True)
                        P.mm(pc[:, 16:32], cb[TP, CB_ONES:CB_ONES + 128], ahi[TP, :], start=True, stop=False)
                        P.mm(pc[:, 16:32], cb[TP, CB_ONES:CB_ONES + 128], alo[TP, :], start=False, stop=True)
                        pb = bC
                        for g in range(2):
                            P.mm(pb[TP, g * 128:(g + 1) * 128], AX_(xc_ap[:, 8 + g, jsl]), identb)
                        for g in range(2):
                            P.mm(pb[TP, 256 + g * nt:256 + (g + 1) * nt], AX_(xc_ap[:, 8 + g, jsl]),
                                 AX_(xc_ap[:, 10 + g, jsl]))
                        for c in range(4):
                            P.mm(bA[TP, c * 128:(c + 1) * 128], AX_(xc_ap[:, c, jsl]), identb)
                        yield
                        P.ts(nacum[TP, :], pc[TP, 0:16], -1.0, ALU.mult)
                        P.act(eacum[TP, :], pc[TP, 0:16], AF.Exp)
                        P.tt(dend[TP, :], pc[TP, 16:32], nacum[TP, :], ALU.add)
                        P.act(dend[TP, :], dend[TP, :], AF.Exp)
                        P.act(cdec[:], pc[:, 16:32], AF.Exp)
                        P.tt(wdt[TP, :], dtt[TP, :], dend[TP, :], ALU.mult)
                        P.cp(Btok[TP, :], pb[TP, 0:256], eng="scalar")
                        P.cp(cbT[TP, 0:2 * nt], pb[TP, 256:256 + 2 * nt], eng="scalar")
                        for hf in range(2):
                            src = r8(bA, 0, 512)
                            P.tt(r8(xdt, hf * 512, (hf + 1) * 512), src, b8(dtt, 8 * hf), ALU.mult)
                            P.tt(r8(xdw, hf * 512, (hf + 1) * 512), src, b8(wdt, 8 * hf), ALU.mult)
                            P.tt(r8(xD, hf * 512, (hf + 1) * 512), src, b8(pvl, PV_DSK + 8 * hf), ALU.mult)
                            if hf == 0:
                                for c in range(4, 8):
                                    P.mm(bA[TP, (c - 4) * 128:(c - 3) * 128], AX_(xc_ap[:, c, jsl]), identb)
                                yield
                        yield
                        for hq in range(4):
                            g = hq // 2
                            sg = bC if hq % 2 == 0 else bB
                            sg3 = sg.v(sg.h[TP, 0:4 * nt].rearrange("p (a b) -> p a b", a=4))
                            P.mm(sg3, identn,
                                 cb.v(cb.h[TP, CB_NEG4:CB_NEG4 + 512].rearrange("p (a b) -> p a b", a=4)[:, :, 0:nt]),
                                 start=True, stop=False)
                            for hh in range(4):
                                h = 4 * hq + hh
                                P.mm(sg[TP, hh * nt:(hh + 1) * nt], ahi.v(ahi.h[TP, h:h + 1].broadcast_to([nt, nt])),
                                     trin, start=False, stop=False)
                                P.mm(sg[TP, hh * nt:(hh + 1) * nt], alo.v(alo.h[TP, h:h + 1].broadcast_to([nt, nt])),
                                     trin, start=False, stop=(hh == 3))
                            yield
                            for hh in range(4):
                                h = 4 * hq + hh
                                P.act(dTt[TP, hh * nt:(hh + 1) * nt], sg[TP, hh * nt:(hh + 1) * nt], AF.Exp,
                                      bias=nacum[TP, h:h + 1])
                            P.tt(MT.v(MT.h[TP, 4 * hq:4 * hq + 4, 0:nt]),
                                 dTt.v(dTt.h[TP, 0:4 * nt].rearrange("p (a b) -> p a b", a=4)),
                                 cbT.v(cbT.h[TP, g * nt:(g + 1) * nt].unsqueeze(1).broadcast_to([nt, 4, nt])),
                                 ALU.mult, eng="gpsimd")
                        yield
                        for zi in range(2):
                            zb = bA if zi == 0 else bC
                            _, zwb, zwv = zpc[zi]
                            for k in range(8):
                                P.mm(zb[TP, :], uT[:, k, jsl], zwb.v(zwv[:, k, :]), start=(k == 0), stop=(k == 7))
                            P.act(zs[TP, zi * 512:(zi + 1) * 512], zb[TP, :], AF.Silu, scale=rstok[TP, j:j + 1])
                        yield
                        while F["h"] != j:
                            yield
                        if is_s:
                            P.dma("sync", v3(stg, 0, 1024, 8), ssd.v(ssd.h[l, b].rearrange("(c p) n -> p c n", p=128)))
                            for hf in range(2):
                                for c in range(4):
                                    P.tr(bB[:, c * 128:(c + 1) * 128], stg[:, (hf * 4 + c) * 128:(hf * 4 + c + 1) * 128],
                                         identf)
                                P.cp(hT[l][:, hf * 512:(hf + 1) * 512], bB[:])
                                P.cp(hTb[l][:, hf * 512:(hf + 1) * 512], bB[:], eng="scalar")
                        for g in range(2):
                            gs = slice(g * 512, (g + 1) * 512)
                            yi, yo, stp = bA, bB, bC
                            for hh in range(8):
                                h = 8 * g + hh
                                P.mm(yi[TP, hh * 64:(hh + 1) * 64], MT[TP, h, 0:nt], xdt[TP, h * 64:(h + 1) * 64],
                                     start=(hh == 0), stop=False, skip_group_check=True)
                            P.mm(yi[TP, :], identn, xD[TP, gs], start=False, stop=True, skip_group_check=True)
                            P.mm(yo[TP, :], AX_(xc_ap[:, 10 + g, jsl]), hTb[l][:, gs])
                            P.mm(stp[:], Btok[TP, g * 128:(g + 1) * 128], xdw[TP, gs])
                            yield
                            yt = ytmp[g]
                            P.tt(r8(yt, 0, 512), r8(yo, 0, 512), b8(eacum, 8 * g), ALU.mult)
                            P.tt(yt[TP, :], yi[TP, :], yt[TP, :], ALU.add)
                            hv = hT[l].v(hT[l].h[:, gs].rearrange("p (a b) -> p a b", a=8))
                            P.tt(hv, hv, cdec.v(cdec.h[:, 8 * g:8 * g + 8].unsqueeze(2).broadcast_to([128, 8, 64])),
                                 ALU.mult, eng="gpsimd")
                            P.tt(hT[l][:, gs], hT[l][:, gs], stp[:], ALU.add)
                            P.cp(hTb[l][:, gs], hT[l][:, gs], eng="scalar")
                            P.tt(yt[TP, :], yt[TP, :], zs[TP, gs], ALU.mult)
                            P.act(sqj[TP, :], yt[TP, :], AF.Square)
                            P.op("vector", "tensor_reduce", [sqj[TP, :]], [ss2[TP, g:g + 1]], out=ss2[TP, g:g + 1],
                                 in_=sqj[TP, :], axis=AX.X, op=ALU.add)
                            yield
                        if emits(j):
                            for hf in range(2):
                                for c in range(4):
                                    P.tr(bC[:, c * 128:(c + 1) * 128], hT[l][:, (hf * 4 + c) * 128:(hf * 4 + c + 1) * 128],
                                         identf)
                                P.cp(stg[:, hf * 512:(hf + 1) * 512], bC[:], eng=("scalar" if hf else "vector"))
                            dsth = o_sh.h[l, b] if is_s else o_ph.h[l]
                            dstT = o_sh if is_s else o_ph
                            P.dma("sync", dstT.v(dsth.rearrange("(c p) n -> p c n", p=128)), v3(stg, 0, 1024, 8))
                        F["h"] = j + 1
                        yield
                        P.act(ss2[TP, :], ss2[TP, :], AF.Ln, scale=1.0 / 512, bias=cf[TP, 128:129])
                        P.act(ss2[TP, :], ss2[TP, :], AF.Exp, scale=-0.5)
                        for g in range(2):
                            P.act(ysm[TP, g * 512:(g + 1) * 512], ytmp[g][TP, :], AF.Copy, scale=ss2[TP, g:g + 1])
                        pxs = [bA, bB]
                        for c in range(8):
                            P.mm(pxs[c // 4][:, (c % 4) * nt:(c % 4 + 1) * nt], ysm[TP, c * 128:(c + 1) * 128], identn)
                        yield
                        for c in range(8):
                            P.act(mixT[:, 4 + c, jsl], pxs[c // 4][:, (c % 4) * nt:(c % 4 + 1) * nt], AF.Copy,
                                  scale=pvl[:, PV_SSMN + c:PV_SSMN + c + 1])
                        yield
                    for zi in range(2):
                        wrel(zpc[zi][0])
                    rel(bk)

                run([qk_thread([0, 2, 4], qksets[0]), qk_thread([1, 3], qksets[1]), bulk_thread(),
                     att_thread(), ssd_thread(list(range(0, NSB, 2)), ssd_sets[0]),
                     ssd_thread(list(range(1, NSB, 2)), ssd_sets[1])])
                assert len(banks_free) == 8
                nxt = (t, l + 1) if l + 1 < NL else (t + 1, 0)
                if nxt[0] < NT and nxt[0] >= NPT:
                    sample_presetup(*nxt)

                def ffn_thread():
                    allb = list(pA) + list(pO) + list(pX)
                    ri = [0]

                    def nb():
                        ri[0] += 1
                        return allb[ri[0] % 8]
                    pj = None
                    for m in range(8):
                        if m % 2 == 0:
                            if pj is not None:
                                wrel(pj)
                            pj, wb, wv = yield from wget(t, l, "O%d" % (m // 2))
                        ps = nb()
                        for c in range(12):
                            P.mm(ps[:, W], wb.v(wv[:, c, (m % 2) * 128:(m % 2 + 1) * 128]), mixT[:, c, W],
                                 start=(c == 0), stop=(c == 11))
                        P.tt(xTk[m][:, W], xTk[m][:, W], ps[:, W], ALU.add)
                    wrel(pj)
                    rmsnorm(PV_GFFN, l, False)
                    for jg in range(6):
                        gj, wgb, wg = yield from wget(t, l, "G%d" % jg)
                        uj, wub, wu = yield from wget(t, l, "U%d" % jg)
                        for ff in range(4 if jg < 5 else 2):
                            fc = jg * 4 + ff
                            fsl = slice(ff * 128, (ff + 1) * 128)
                            pg = nb()
                            for k in range(8):
                                P.mm(pg[:, W], wgb.v(wg[:, k, fsl]), uT[:, k, W], start=(k == 0), stop=(k == 7))
                            pu = nb()
                            for k in range(8):
                                P.mm(pu[:, W], wub.v(wu[:, k, fsl]), uT[:, k, W], start=(k == 0), stop=(k == 7))
                            ts_ = t2 if fc % 2 == 0 else t2b
                            tu_ = t1 if fc % 2 == 0 else t1b
                            P.tt(ts_[:, W], pg[:, W], rs[:, W], ALU.mult)
                            P.tt(tu_[:, W], pu[:, W], rs[:, W], ALU.mult, eng="vector")
                            P.act(ts_[:, W], ts_[:, W], AF.Silu)
                            P.tt(A(act_ap[:, fc, W]), ts_[:, W], tu_[:, W], ALU.mult, xw=[xc_dummy])
                        wrel(gj)
                        wrel(uj)
                    if not (t == NT - 1 and l == NL - 1):
                        build_diagw((l + 1) % NL)
                    for m in range(8):
                        pj, wb, wv = yield from wget(t, l, "N%d" % m)
                        ps = nb()
                        for fc in range(22):
                            P.mm(ps[:, W], wb.v(wv[:, fc, :]), A(act_ap[:, fc, W]), start=(fc == 0), stop=(fc == 21),
                                 xr=[xc_dummy])
                        P.tt(xTk[m][:, W], xTk[m][:, W], ps[:, W], ALU.add)
                        wrel(pj)
                        if l == NL - 1:
                            P.dma("sync", yout[m * 128:(m + 1) * 128, tok0:tok0 + TW], xTk[m][:, W])
                run([ffn_thread()])
        P.emit()
    return nc


def _consts():
    bf = ml_dtypes.bfloat16
    i = np.arange(128)
    ident = np.eye(128, dtype=np.float32)
    tri = (i[:, None] <= i[None, :]).astype(np.float32)
    ones = np.ones((128, 128), np.float32)
    neg = np.where(i[:, None] > i[None, :], -30000.0, 0.0).astype(np.float32)
    neg4 = np.tile(neg, (1, 4))
    mprev = (i[:, None] > i[None, :]).astype(np.float32)
    mcur = (i[:, None] <= i[None, :]).astype(np.float32)
    blk = ((i[:, None] // 64) == (i[None, :] // 64)).astype(np.float32)
    rot = np.zeros((128, 128), np.float32)
    for m in range(128):
        h, d = m // 64, m % 64
        if d < 32:
            rot[h * 64 + d + 32, m] = -1.0
        else:
            rot[h * 64 + d - 32, m] = 1.0
    negp4 = np.tile(np.where(i[:, None] <= i[None, :], -30000.0, 0.0).astype(np.float32), (1, 4))
    cbf = np.concatenate([ident, tri, ones, neg4, mprev, mcur, blk, rot, negp4], axis=1).astype(bf)
    assert cbf.shape[1] == NCB
    cf = np.zeros((128, NCF), np.float32)
    cf[:, 0:128] = ident
    cf[:, 128] = EPS
    cf[:, 129] = 1.0
    cf[:4, 130] = 1.0
    return cbf, cf


def _rope_tables(pos):
    half = HD // 2
    inv = (np.float32(10000.0) ** (-np.arange(half, dtype=np.float32) / np.float32(half))).astype(np.float32)
    ang = pos.astype(np.float32)[None, :] * inv[:, None]
    cos = np.cos(ang).astype(np.float32)
    sin = np.sin(ang).astype(np.float32)
    idx = (np.arange(128) % 64) % 32
    return np.ascontiguousarray(cos[idx]), np.ascontiguousarray(sin[idx])


def _pvec(inp, l):
    pvv = np.zeros((128, NPV), np.float32)
    p = np.arange(128)
    pvv[:, PV_GMIX:PV_GMIX + 8] = inp["norm_mix"][l].reshape(8, 128).T
    pvv[:, PV_GFFN:PV_GFFN + 8] = inp["norm_ffn"][l].reshape(8, 128).T
    pvv[:, PV_QG] = inp["q_norm"][l][p % 64]
    pvv[:, PV_KG] = inp["k_norm"][l][p % 64]
    pvv[:, PV_SINK:PV_SINK + 8] = inp["sinks"][l][None, :]
    cw = inp["conv_w"][l]
    pvv[:, PV_CW:PV_CW + 48] = cw.reshape(4, 12, 128).transpose(2, 1, 0).reshape(128, 48)
    pvv[:, PV_CB:PV_CB + 12] = inp["conv_b"][l].reshape(12, 128).T
    pvv[:, PV_DTB:PV_DTB + 16] = inp["dt_bias"][l][None, :]
    pvv[:, PV_ALOG:PV_ALOG + 16] = inp["a_log"][l][None, :]
    pvv[:, PV_DSK:PV_DSK + 16] = inp["d_skip"][l][None, :]
    pvv[:, PV_SSMN:PV_SSMN + 8] = inp["ssm_norm"][l].reshape(8, 128).T
    return pvv


_NC_CACHE = {}


def run_kernel(inp, n_cores):
    inp = {k: np.asarray(v) for k, v in inp.items()}
    xp, xs = inp["x_prompt"], inp["x_sample"]
    BP, SEQ_, _ = xp.shape
    BS, TS, _ = xs.shape
    assert SEQ_ % 512 == 0 and TS == 4 and BS % (4 * n_cores) == 0
    NPT = SEQ_ // 512
    NSEQ = BS // n_cores
    NST = 1
    NTOK = NPT * 512 + NSEQ * 4
    key = (NPT, NST, NSEQ)
    if key not in _NC_CACHE:
        _NC_CACHE[key] = build(NPT, NST, NSEQ)
    nc = _NC_CACHE[key]
    cbf, cf = _consts()
    pos = np.zeros(NTOK, np.float32)
    pos[:SEQ_] = np.arange(SEQ_)
    for s in range(NSEQ):
        pos[SEQ_ + s * 4:SEQ_ + s * 4 + 4] = PAST + np.arange(4)
    cosT, sinT = _rope_tables(pos)
    pvec = np.stack([_pvec(inp, l) for l in range(NL)])
    shared = {"cosT": cosT, "sinT": sinT, "pvec": pvec, "cbf": cbf, "cf32": cf,
              "w_in": np.ascontiguousarray(inp["w_in"]), "w_out": np.ascontiguousarray(inp["w_out"]),
              "w_gu": np.ascontiguousarray(inp["w_gate_up"]), "w_dn": np.ascontiguousarray(inp["w_down"])}
    in_maps = []
    for c in range(n_cores):
        sq_ = c % BP
        b0 = c * NSEQ
        xT = np.zeros((D, NTOK), np.float32)
        xT[:, :SEQ_] = xp[sq_].T
        xT[:, SEQ_:] = xs[b0:b0 + NSEQ].reshape(NSEQ * 4, D).T
        m = dict(shared)
        m["xT_in"] = xT
        m["cache_k"] = np.ascontiguousarray(inp["cache_win_k"][:, b0:b0 + NSEQ]).reshape(NL, NSEQ, 128, 128)
        m["cache_v"] = np.ascontiguousarray(inp["cache_win_v"][:, b0:b0 + NSEQ]).reshape(NL, NSEQ, 128, 128)
        m["st_conv"] = np.ascontiguousarray(inp["state_conv"][:, b0:b0 + NSEQ])
        m["st_ssm"] = np.ascontiguousarray(inp["state_ssm"][:, b0:b0 + NSEQ]).reshape(NL, NSEQ, 1024, 128)
        in_maps.append(m)
    res = run_bass_kernel_spmd(nc, in_maps, core_ids=list(range(n_cores)))
    R = res.results
    y_p = np.stack([R[s]["yT"][:, :SEQ_].T for s in range(BP)])
    y_s = np.concatenate([R[c]["yT"][:, SEQ_:].reshape(D, NSEQ, 4).transpose(1, 2, 0)
                          for c in range(n_cores)], axis=0)
    pk = np.stack([R[s]["o_pk"] for s in range(BP)], axis=1).reshape(NL, BP, 128, 2, 64)
    pvv = np.stack([R[s]["o_pv"] for s in range(BP)], axis=1).reshape(NL, BP, 128, 2, 64)
    pcv = np.stack([R[s]["o_pc"] for s in range(BP)], axis=1)
    ph = np.stack([R[s]["o_ph"] for s in range(BP)], axis=1).reshape(NL, BP, 16, 64, 128)
    sk = np.concatenate([R[c]["o_sk"] for c in range(n_cores)], axis=1).reshape(NL, BS, 128, 2, 64)
    sv = np.concatenate([R[c]["o_sv"] for c in range(n_cores)], axis=1).reshape(NL, BS, 128, 2, 64)
    sc = np.concatenate([R[c]["o_sc"] for c in range(n_cores)], axis=1)
    sh = np.concatenate([R[c]["o_sh"] for c in range(n_cores)], axis=1).reshape(NL, BS, 16, 64, 128)
    outs = (y_p, y_s, pk, pvv, pcv, ph, sk, sv, sc, sh)
    return tuple(np.ascontiguousarray(o, dtype=np.float32) for o in outs)


def kernel(**inputs):
    return run_kernel(inputs, 8)
```
